# Optimizing a Trainium2 kernel written in Bass

```python
import jax, jax.numpy as jnp
from jax import lax
import numpy as np

D_MODEL = 2048
BATCH = 4
SEQ = 8192
DEPTH = 1
DEC_BATCH = 32
DEC_SEQ = 32
PAST_LEN = 1024

CHUNK = 64
SB_HEAD_DIM = 128
SB_HEADS = D_MODEL // (2 * SB_HEAD_DIM)
SB_BLOCK = 128
GLA_HEADS = 4
GLA_DK = D_MODEL // (2 * GLA_HEADS)
GLA_DV = D_MODEL // GLA_HEADS
GLA_LOW_RANK = 16
GLA_GATE_NORM = 16.0
MEM_TOKENS = 256
MEM_HEADS = 4
MEM_HEAD_DIM = D_MODEL // (2 * MEM_HEADS)
D_FF = ((8 * D_MODEL // 3 + 255) // 256) * 256
N_BRANCH = 3
EPS = 1e-6

SB_W = SB_HEADS * SB_HEAD_DIM
GLA_KW = GLA_HEADS * GLA_DK
GLA_VW = GLA_HEADS * GLA_DV
MEM_W = MEM_HEADS * MEM_HEAD_DIM
IN_SPLIT = (SB_W, SB_W, SB_W, GLA_KW, GLA_KW, GLA_VW, GLA_VW, GLA_LOW_RANK, MEM_W)
IN_W = SB_W * 3 + GLA_KW * 2 + GLA_VW * 2 + GLA_LOW_RANK + MEM_W

kernel_name = 'streaming_sb_gla_mem_hybrid_step'


def rms_norm(x, g):
    x32 = x.astype(jnp.float32)
    y = x32 * lax.rsqrt(jnp.mean(x32 * x32, axis=-1, keepdims=True) + EPS)
    return (y * g.astype(jnp.float32)).astype(x.dtype)


def swiglu(x, w_gu, w_d):
    g, u = jnp.split(x @ w_gu, 2, axis=-1)
    return (jax.nn.silu(g) * u) @ w_d


def sb_attend(q, k, v, q_pos, k_pos):
    z = jnp.einsum('bqhd,bkhd->bhqk', q.astype(jnp.float32), k.astype(jnp.float32)) * SB_HEAD_DIM ** -0.5
    mask = k_pos[None, :] < q_pos[:, None]
    c = jnp.where(mask, jax.nn.log_sigmoid(-z), 0.0)
    log_a = z + lax.cumsum(c, axis=3, reverse=True)
    a = jnp.exp(jnp.where(mask, log_a, -jnp.inf))
    return jnp.einsum('bhqk,bkhd->bqhd', a, v.astype(jnp.float32))


def sb_prompt(q, k, v):
    B, T = q.shape[0], q.shape[1]
    nb = T // SB_BLOCK
    qb = q.reshape(B, nb, SB_BLOCK, SB_HEADS, SB_HEAD_DIM).swapaxes(0, 1)
    starts = jnp.arange(nb, dtype=jnp.int32) * SB_BLOCK
    k_pos = jnp.arange(T, dtype=jnp.int32)

    def one_block(args):
        qi, s0 = args
        return sb_attend(qi, k, v, s0 + jnp.arange(SB_BLOCK, dtype=jnp.int32), k_pos)

    o = lax.map(one_block, (qb, starts))
    return o.swapaxes(0, 1).reshape(B, T, SB_HEADS, SB_HEAD_DIM)


def gla_chunk(s, qc, kc, vc, gc):
    L = qc.shape[1]
    b = jnp.cumsum(gc, axis=1)
    o_inter = jnp.einsum('blhd,bhde->blhe', qc * jnp.exp(b), s)
    causal = jnp.tril(jnp.ones((L, L), dtype=bool))[None, :, :, None, None]
    diff = b[:, :, None] - b[:, None, :]
    decay = jnp.where(causal, jnp.exp(jnp.where(causal, diff, 0.0)), 0.0)
    att = jnp.einsum('bthd,bshd,btshd->bhts', qc, kc, decay)
    o_intra = jnp.einsum('bhts,bshe->bthe', att, vc)
    b_last = b[:, -1]
    s_new = jnp.exp(b_last)[..., None] * s + jnp.einsum('bshd,bshe->bhde', kc * jnp.exp(b_last[:, None] - b), vc)
    return s_new, o_inter + o_intra


def gla_prompt(q, k, v, g):
    B, T = q.shape[0], q.shape[1]
    n = T // CHUNK

    def to_chunks(a):
        return a.reshape(B, n, CHUNK, a.shape[2], a.shape[3]).swapaxes(0, 1)

    s0 = jnp.zeros((B, GLA_HEADS, GLA_DK, GLA_DV), jnp.float32)
    s_fin, o = lax.scan(lambda s, xs: gla_chunk(s, xs[0], xs[1], xs[2], xs[3]), s0,
                        (to_chunks(q), to_chunks(k), to_chunks(v), to_chunks(g)))
    return s_fin, o.swapaxes(0, 1).reshape(B, T, GLA_HEADS, GLA_DV)


def mem_attend(q, mk, mv):
    s = jnp.einsum('bthd,bmhd->bhtm', q.astype(jnp.float32), mk.astype(jnp.float32)) * MEM_HEAD_DIM ** -0.5
    p = jax.nn.softmax(s, axis=-1)
    return jnp.einsum('bhtm,bmhd->bthd', p, mv.astype(jnp.float32))


def mixer(u, p, mem_k, mem_v, sb_past_k, sb_past_v, gla_s0):
    B, T = u.shape[0], u.shape[1]
    dt = u.dtype
    cuts = [int(c) for c in np.cumsum(IN_SPLIT)[:-1]]
    sq, sk, sv, gq, gk, gv, gr, ga, mq = jnp.split(u @ p['w_in'], cuts, axis=-1)

    def heads(a, h):
        return a.reshape(B, T, h, a.shape[-1] // h)

    sq, sk, sv = heads(sq, SB_HEADS), heads(sk, SB_HEADS), heads(sv, SB_HEADS)
    if sb_past_k is None:
        o_sb = sb_prompt(sq, sk, sv)
    else:
        P = sb_past_k.shape[1]
        keys = jnp.concatenate([sb_past_k.astype(dt), sk], axis=1)
        vals = jnp.concatenate([sb_past_v.astype(dt), sv], axis=1)
        o_sb = sb_attend(sq, keys, vals, P + jnp.arange(T, dtype=jnp.int32), jnp.arange(P + T, dtype=jnp.int32))
    g_log = jax.nn.log_sigmoid((ga @ p['gla_w_a2'] + p['gla_b_a2']).astype(jnp.float32)) / GLA_GATE_NORM
    g_log = heads(g_log, GLA_HEADS)
    qg = heads(gq, GLA_HEADS).astype(jnp.float32) * GLA_DK ** -0.5
    kg = heads(gk, GLA_HEADS).astype(jnp.float32)
    vg = heads(gv, GLA_HEADS).astype(jnp.float32)
    if gla_s0 is None:
        s_new, o_gla = gla_prompt(qg, kg, vg, g_log)
    else:
        s_new, o_gla = gla_chunk(gla_s0.astype(jnp.float32), qg, kg, vg, g_log)
    o_gla = rms_norm(o_gla, p['gla_norm_g']) * jax.nn.silu(heads(gr, GLA_HEADS).astype(jnp.float32))
    o_mem = mem_attend(heads(mq, MEM_HEADS), mem_k, mem_v)
    br_sb = o_sb.reshape(B, T, SB_W).astype(dt) @ p['w_sb_br']
    br_gla = o_gla.reshape(B, T, GLA_VW).astype(dt) @ p['w_gla_br']
    br_mem = o_mem.reshape(B, T, MEM_W).astype(dt) @ p['w_mem_br']
    gates = jax.nn.sigmoid(u @ p['w_gate'] + p['b_gate']).reshape(B, T, N_BRANCH, D_MODEL)
    merged = gates[:, :, 0] * br_sb + gates[:, :, 1] * br_gla + gates[:, :, 2] * br_mem
    return merged @ p['w_out'], sk, sv, s_new.astype(dt)


def layer(x, p, mem_k, mem_v, sb_past_k, sb_past_v, gla_s0):
    h = x + 0.5 * rms_norm(swiglu(rms_norm(x, p['ffn1_pre_g']), p['ffn1_w_gu'], p['ffn1_w_d']), p['ffn1_post_g'])
    m, sk, sv, s_new = mixer(rms_norm(h, p['mix_pre_g']), p, mem_k, mem_v, sb_past_k, sb_past_v, gla_s0)
    h = h + rms_norm(m, p['mix_post_g'])
    h = h + 0.5 * rms_norm(swiglu(rms_norm(h, p['ffn2_pre_g']), p['ffn2_w_gu'], p['ffn2_w_d']), p['ffn2_post_g'])
    return h, sk, sv, s_new


def setup_inputs(seed: int = 0) -> dict:
    key = jax.random.key(seed)
    ks = jax.random.split(key, 32)
    f32 = jnp.float32

    def nrm(k, shape, scale):
        return jax.random.normal(k, shape, f32) * scale

    def gain(k, n):
        return 1.0 + 0.02 * jax.random.normal(k, (DEPTH, n), f32)

    return {
        'x_prompt': nrm(ks[0], (BATCH, SEQ, D_MODEL), 1.0),
        'x_sample': nrm(ks[1], (DEC_BATCH, DEC_SEQ, D_MODEL), 1.0),
        'mem_prompt': nrm(ks[2], (BATCH, MEM_TOKENS, D_MODEL), 1.0),
        'cache_sb_k': nrm(ks[3], (DEPTH, DEC_BATCH, PAST_LEN, SB_HEADS, SB_HEAD_DIM), 1.0),
        'cache_sb_v': nrm(ks[4], (DEPTH, DEC_BATCH, PAST_LEN, SB_HEADS, SB_HEAD_DIM), 1.0),
        'state_gla': nrm(ks[5], (DEPTH, DEC_BATCH, GLA_HEADS, GLA_DK, GLA_DV), 1.0),
        'cache_mem_k': nrm(ks[6], (DEPTH, DEC_BATCH, MEM_TOKENS, MEM_HEADS, MEM_HEAD_DIM), 1.0),
        'cache_mem_v': nrm(ks[7], (DEPTH, DEC_BATCH, MEM_TOKENS, MEM_HEADS, MEM_HEAD_DIM), 1.0),
        'ffn1_pre_g': gain(ks[8], D_MODEL),
        'ffn1_w_gu': nrm(ks[9], (DEPTH, D_MODEL, 2 * D_FF), D_MODEL ** -0.5),
        'ffn1_w_d': nrm(ks[10], (DEPTH, D_FF, D_MODEL), D_FF ** -0.5),
        'ffn1_post_g': gain(ks[11], D_MODEL),
        'mix_pre_g': gain(ks[12], D_MODEL),
        'w_in': nrm(ks[13], (DEPTH, D_MODEL, IN_W), D_MODEL ** -0.5),
        'gla_w_a2': nrm(ks[14], (DEPTH, GLA_LOW_RANK, GLA_KW), GLA_LOW_RANK ** -0.5),
        'gla_b_a2': nrm(ks[15], (DEPTH, GLA_KW), 0.1),
        'gla_norm_g': gain(ks[16], GLA_DV),
        'mem_norm_g': gain(ks[17], D_MODEL),
        'w_mem_kv': nrm(ks[18], (DEPTH, D_MODEL, 2 * MEM_W), D_MODEL ** -0.5),
        'w_sb_br': nrm(ks[19], (DEPTH, SB_W, D_MODEL), SB_W ** -0.5),
        'w_gla_br': nrm(ks[20], (DEPTH, GLA_VW, D_MODEL), GLA_VW ** -0.5),
        'w_mem_br': nrm(ks[21], (DEPTH, MEM_W, D_MODEL), MEM_W ** -0.5),
        'w_gate': nrm(ks[22], (DEPTH, D_MODEL, N_BRANCH * D_MODEL), D_MODEL ** -0.5),
        'b_gate': nrm(ks[23], (DEPTH, N_BRANCH * D_MODEL), 0.02),
        'w_out': nrm(ks[24], (DEPTH, D_MODEL, D_MODEL), D_MODEL ** -0.5),
        'mix_post_g': gain(ks[25], D_MODEL),
        'ffn2_pre_g': gain(ks[26], D_MODEL),
        'ffn2_w_gu': nrm(ks[27], (DEPTH, D_MODEL, 2 * D_FF), D_MODEL ** -0.5),
        'ffn2_w_d': nrm(ks[28], (DEPTH, D_FF, D_MODEL), D_FF ** -0.5),
        'ffn2_post_g': gain(ks[29], D_MODEL),
    }


def reference(x_prompt, x_sample, mem_prompt, cache_sb_k, cache_sb_v, state_gla, cache_mem_k, cache_mem_v,
              ffn1_pre_g, ffn1_w_gu, ffn1_w_d, ffn1_post_g, mix_pre_g, w_in, gla_w_a2, gla_b_a2, gla_norm_g,
              mem_norm_g, w_mem_kv, w_sb_br, w_gla_br, w_mem_br, w_gate, b_gate, w_out, mix_post_g,
              ffn2_pre_g, ffn2_w_gu, ffn2_w_d, ffn2_post_g):
    h_p, h_s = x_prompt, x_sample
    Bp = x_prompt.shape[0]
    sbk_p, sbv_p, gla_p, memk_p, memv_p = [], [], [], [], []
    sbk_s, sbv_s, gla_s = [], [], []
    for l in range(DEPTH):
        p = dict(ffn1_pre_g=ffn1_pre_g[l], ffn1_w_gu=ffn1_w_gu[l], ffn1_w_d=ffn1_w_d[l], ffn1_post_g=ffn1_post_g[l],
                 mix_pre_g=mix_pre_g[l], w_in=w_in[l], gla_w_a2=gla_w_a2[l], gla_b_a2=gla_b_a2[l],
                 gla_norm_g=gla_norm_g[l], w_sb_br=w_sb_br[l], w_gla_br=w_gla_br[l], w_mem_br=w_mem_br[l],
                 w_gate=w_gate[l], b_gate=b_gate[l], w_out=w_out[l], mix_post_g=mix_post_g[l],
                 ffn2_pre_g=ffn2_pre_g[l], ffn2_w_gu=ffn2_w_gu[l], ffn2_w_d=ffn2_w_d[l], ffn2_post_g=ffn2_post_g[l])
        mk, mv = jnp.split(rms_norm(mem_prompt, mem_norm_g[l]) @ w_mem_kv[l], 2, axis=-1)
        mk = mk.reshape(Bp, MEM_TOKENS, MEM_HEADS, MEM_HEAD_DIM)
        mv = mv.reshape(Bp, MEM_TOKENS, MEM_HEADS, MEM_HEAD_DIM)
        h_p, k_p, v_p, s_p = layer(h_p, p, mk, mv, None, None, None)
        h_s, k_s, v_s, s_s = layer(h_s, p, cache_mem_k[l], cache_mem_v[l], cache_sb_k[l], cache_sb_v[l], state_gla[l])
        sbk_p.append(k_p); sbv_p.append(v_p); gla_p.append(s_p); memk_p.append(mk); memv_p.append(mv)
        sbk_s.append(k_s); sbv_s.append(v_s); gla_s.append(s_s)
    return (h_p, h_s, jnp.stack(sbk_p), jnp.stack(sbv_p), jnp.stack(gla_p), jnp.stack(memk_p), jnp.stack(memv_p),
            jnp.stack(sbk_s), jnp.stack(sbv_s), jnp.stack(gla_s))
```

```python
import contextlib
import numpy as np
import concourse.bass as bass
import concourse.mybir as mybir
from concourse.bass_utils import run_bass_kernel_spmd

F32 = mybir.dt.float32
BF16 = mybir.dt.bfloat16
AF = mybir.ActivationFunctionType
ALU = mybir.AluOpType

D = 2048
KC = 16
DFF = 5632
NJ = 44
EPS = 1e-6
ENGS = ("pe", "act", "dve", "pool", "sp")
NDMA_SEMS = 12


class Buf:
    __slots__ = ("name", "w", "r")

    def __init__(self, name):
        self.name = name
        self.w = None
        self.r = {}


class Prog:
    def __init__(self, nc):
        self.nc = nc
        self.ops = {e: [] for e in ENGS}
        self.cnt = {}
        self.seen = {e: {} for e in ENGS}
        self.nd = {}
        self.eng_keys = {}
        self.dma_keys = []

    def _deps(self, eng, reads, writes):
        need = {}

        def add(tok):
            if tok is None:
                return
            k, c = tok
            if need.get(k, 0) < c:
                need[k] = c

        for b in reads:
            add(b.w)
        for b in writes:
            add(b.w)
            for k, c in b.r.items():
                add((k, c))
        waits = []
        seen = self.seen[eng]
        for k, c in need.items():
            if eng == "pe" and k == "pe":
                continue
            if seen.get(k, 0) < c:
                seen[k] = c
                waits.append((k, c))
        return waits

    def _commit(self, tok, reads, writes):
        k, c = tok
        for b in reads:
            if b.r.get(k, 0) < c:
                b.r[k] = c
        for b in writes:
            b.w = tok
            b.r = {}

    def op(self, eng, meth, *args, reads=(), writes=(), **kw):
        fn = (meth, args, kw)
        waits = self._deps(eng, reads, writes)
        c = self.cnt.get(eng, 0) + 1
        self.cnt[eng] = c
        self.ops[eng].append((waits, fn, (eng, 1)))
        tok = (eng, c)
        self._commit(tok, reads, writes)
        return tok

    def op_nosig(self, eng, meth, *args, reads=(), writes=(), **kw):
        fn = (meth, args, kw)
        waits = self._deps(eng, reads, writes)
        c = self.cnt.get(eng, 0) + 1
        self.ops[eng].append((waits, fn, None))
        tok = (eng, c)
        self._commit(tok, reads, writes)
        return tok

    def dma(self, eng, out, in_, reads=(), writes=(), **kw):
        fn = ("dma_start", (), dict(out=out, in_=in_, **kw))
        waits = self._deps(eng, reads, writes)
        n = self.nd.get(eng, 0)
        self.nd[eng] = n + 1
        key = "d_%s_%d" % (eng, n % NDMA_SEMS)
        if key not in self.dma_keys:
            self.dma_keys.append(key)
            self.eng_keys.setdefault(eng, []).append(key)
        prev = self.cnt.get(key, 0)
        if prev and self.seen[eng].get(key, 0) < prev:
            self.seen[eng][key] = prev
            waits.append((key, prev))
        c = prev + 16
        self.cnt[key] = c
        self.ops[eng].append((waits, fn, (key, 16)))
        tok = (key, c)
        self._commit(tok, reads, writes)
        return tok

    def finish(self):
        for eng, keys in self.eng_keys.items():
            waits = [(k, self.cnt[k]) for k in keys if self.seen[eng].get(k, 0) < self.cnt[k]]
            self.ops[eng].append((waits, None, None))

    def check(self):
        val = {}
        pc = {e: 0 for e in ENGS}
        progress = True
        while progress:
            progress = False
            for e in ENGS:
                ops = self.ops[e]
                while pc[e] < len(ops):
                    waits, fn, sig = ops[pc[e]]
                    if any(val.get(k, 0) < c for k, c in waits):
                        break
                    if sig is not None:
                        val[sig[0]] = val.get(sig[0], 0) + sig[1]
                    pc[e] += 1
                    progress = True
        stuck = {e: (pc[e], len(self.ops[e])) for e in ENGS if pc[e] < len(self.ops[e])}
        if stuck:
            msg = []
            for e, (i, n) in stuck.items():
                waits = self.ops[e][i][0]
                msg.append("%s@%d/%d waits %s have %s" % (e, i, n, waits, [(k, val.get(k, 0)) for k, _ in waits]))
            raise RuntimeError("DEADLOCK: " + "; ".join(msg))

    def emit(self):
        self.check()
        nc = self.nc
        keys = list(ENGS) + self.dma_keys
        with contextlib.ExitStack() as st:
            sems = {k: st.enter_context(nc.semaphore("s_" + k)) for k in keys}
            block = st.enter_context(nc.Block())

            def run(engname):
                def body(e):
                    for waits, fn, sig in self.ops[engname]:
                        for k, c in waits:
                            e.wait_ge(sems[k], c)
                        if fn is not None:
                            ins = getattr(e, fn[0])(*fn[1], **fn[2])
                            if sig is not None:
                                ins.then_inc(sems[sig[0]], sig[1])
                return body

            block.tensor(run("pe"))
            block.scalar(run("act"))
            block.vector(run("dve"))
            block.gpsimd(run("pool"))
            block.sync(run("sp"))


class Builder:
    def __init__(self, seq=8192, past=1024, stages="all"):
        self.SEQ = seq
        self.PAST = past
        self.NS = 4
        self.SD = 32
        self.NT = seq // 512
        self.stages = stages

    def sb(self, name, shape, dt=F32):
        return self.st.enter_context(self.nc.sbuf_tensor(name, shape, dt))[:]

    def din(self, name, shape, dt=F32):
        return self.nc.dram_tensor(name, list(shape), dt, kind="ExternalInput").ap()

    def dout(self, name, shape, dt=F32):
        return self.nc.dram_tensor(name, list(shape), dt, kind="ExternalOutput").ap()

    def dscr(self, name, shape, dt=BF16):
        return self.nc.dram_tensor(name, list(shape), dt, kind="Internal").ap()

    def next_ps(self):
        i = self.ps_i % 4
        self.ps_i += 1
        return self.ps[i], self.b_ps[i]

    def t32(self):
        i = self.r32_i % len(self.b_r32)
        self.r32_i += 1
        return self.r32[:, i, :], self.b_r32[i]

    def t16(self):
        i = self.r16_i % len(self.b_r16)
        self.r16_i += 1
        return self.r16[:, i, :], self.b_r16[i]

    def evac_eng(self):
        self.ev_i += 1
        return "act" if self.ev_i % 2 else "dve"

    def copy(self, eng, out, in_, reads, writes):
        if eng == "act":
            return self.P.op("act", "copy", out, in_, reads=reads, writes=writes)
        return self.P.op(eng, "tensor_copy", out, in_, reads=reads, writes=writes)

    def act(self, out, in_, func, reads, writes, **kw):
        return self.P.op("act", "activation", out, in_, func, reads=reads, writes=writes, **kw)

    def mm(self, out, b_out, terms, allsig=False):
        n = len(terms)
        for i, (l, r, rb) in enumerate(terms):
            last = i == n - 1
            f = self.P.op if (last or allsig) else self.P.op_nosig
            f("pe", "matmul", out, l, r, start=(i == 0), stop=last, reads=rb, writes=[b_out])

    def bigc(self, c, T=512):
        return self.big[:, c, 0:T], [self.b_big[c]]

    def big32(self, q, T=512):
        return self.big32v[:, q, 0:T], [self.b_big[2 * q], self.b_big[2 * q + 1]]

    def dT16(self, c, T=512):
        return self.dT16v[:, c, 0:T], [self.b_dT[c // 2]]

    def conv_weight(self, name, src, K, cols):
        nk = K // 128
        dst = self.dscr(name + "_bf", [len(cols), 128, nk, 128])
        bufs = []
        for f, c0 in enumerate(cols):
            b = Buf("%s_p%d" % (name, f))
            s = src[:, c0:c0 + 128].rearrange("(kc p) c -> p kc c", p=128)
            self.conv_pending.append((dst[f], s, b))
            bufs.append(b)
        return (dst, bufs, nk)

    def flush_conv(self, n=None):
        k = len(self.conv_pending) if n is None else min(n, len(self.conv_pending))
        for (d, s, b) in self.conv_pending[:k]:
            self.P.dma("pool", d, s, writes=[b])
        self.conv_pending = self.conv_pending[k:]

    def panel(self, w, f, k0=0, k1=None):
        dst, bufs, nk = w
        if k1 is None:
            k1 = nk
        i = self.ws_i % len(self.b_ws)
        self.ws_i += 1
        n = k1 - k0
        slot = self.ws[:, i, 0:n * 128].rearrange("p (k c) -> p k c", c=128)
        self.P.dma("sp", slot, dst[f][:, k0:k1, :], reads=[bufs[f]], writes=[self.b_ws[i]])
        return slot, self.b_ws[i]

    def proj_fm(self, w, f, T, k0=0, k1=None, src=None, b_src=None, src_k0=0, c0=0):
        if src is None:
            src, b_src = self.nT, self.b_nT
        slot, b_slot = self.panel(w, f, k0, k1)
        nk = slot.shape[1]
        ps, b_p = self.next_ps()
        self.mm(ps[:, 0:T], b_p,
                [(slot[:, k, :], src[:, src_k0 + k, c0:c0 + T], [b_slot, b_src[src_k0 + k]]) for k in range(nk)])
        return ps, b_p

    def proj_tm1(self, w, f, T):
        slot, b_slot = self.panel(w, f)
        ps, b_p = self.next_ps()
        for nb in range(T // 128):
            self.mm(ps[:, nb * 128:(nb + 1) * 128], b_p,
                    [(self.nT[:, k, nb * 128:(nb + 1) * 128], slot[:, k, :], [b_slot, self.b_nT[k]]) for k in range(KC)])
        return ps, b_p

    def proj_tm4(self, w, fs, T):
        NB = T // 128
        banks = [self.next_ps() for _ in range(NB)]
        for q, f in enumerate(fs):
            slot, b_slot = self.panel(w, f)
            for nb in range(NB):
                ps, b_p = banks[nb]
                self.mm(ps[:, q * 128:(q + 1) * 128], b_p,
                        [(self.nT[:, k, nb * 128:(nb + 1) * 128], slot[:, k, :], [b_slot, self.b_nT[k]]) for k in range(KC)])
        return banks

    def rstd_from(self, chunks, T, ones):
        P = self.P
        ps, b_p = self.next_ps()
        n = len(chunks)
        for k, (ap, bl) in enumerate(chunks):
            sq, b_sq = self.t16()
            self.act(sq[:, 0:T], ap, AF.Square, reads=bl, writes=[b_sq])
            P.op("pe", "matmul", ps[:, 0:T], ones, sq[:, 0:T], start=(k == 0), stop=(k == n - 1),
                 reads=[b_sq, self.b_const], writes=[b_p])
        self.act(self.rstd[:, 0:T], ps[:, 0:T], AF.Sqrt, reads=[b_p], writes=[self.b_rstd], bias=EPS)
        P.op("dve", "reciprocal", self.rstd[:, 0:T], self.rstd[:, 0:T], reads=[self.b_rstd], writes=[self.b_rstd])
        return self.rstd, self.b_rstd

    def norm_to_nT(self, gcol, T):
        P = self.P
        rstd, b_r = self.rstd_from([(self.xT[:, k, 0:T], [self.b_xT[k]]) for k in range(KC)], T, self.onesD)
        for k in range(KC):
            P.op("dve", "scalar_tensor_tensor", self.nT[:, k, 0:T], self.xT[:, k, 0:T], gcol[:, k:k + 1], rstd[:, 0:T],
                 op0=ALU.mult, op1=ALU.mult, reads=[self.b_xT[k], b_r, self.b_const], writes=[self.b_nT[k]])

    def postnorm_add(self, chunks, gcol, T):
        P = self.P
        rstd, b_r = self.rstd_from(chunks, T, self.onesD)
        for k, (ap, bl) in enumerate(chunks):
            tmp, b_t = self.t32()
            P.op("pool", "tensor_tensor", tmp[:, 0:T], ap, rstd[:, 0:T], op=ALU.mult, reads=bl + [b_r], writes=[b_t])
            P.op("dve", "scalar_tensor_tensor", self.xT[:, k, 0:T], tmp[:, 0:T], gcol[:, k:k + 1], self.xT[:, k, 0:T],
                 op0=ALU.mult, op1=ALU.add, reads=[b_t, self.b_xT[k], self.b_const], writes=[self.b_xT[k]])

    def ffn(self, wgu, wd, T):
        P = self.P
        for half in range(2):
            for jj in range(22):
                j = half * 22 + jj
                pg, b_pg = self.proj_fm(wgu, j, T)
                pu, b_pu = self.proj_fm(wgu, NJ + j, T)
                sg, b_sg = self.t32()
                self.act(sg[:, 0:T], pg[:, 0:T], AF.Silu, reads=[b_pg], writes=[b_sg])
                P.op("dve", "tensor_tensor", self.big[:, jj, 0:T], sg[:, 0:T], pu[:, 0:T], op=ALU.mult,
                     reads=[b_sg, b_pu], writes=[self.b_big[jj]])
            for o in range(KC):
                pd, b_pd = self.proj_fm(wd, o, T, k0=half * 22, k1=half * 22 + 22, src=self.big, b_src=self.b_big)
                if half == 0:
                    self.copy("act", self.dT[:, o, 0:T], pd[:, 0:T], [b_pd], [self.b_dT[o]])
                else:
                    P.op("dve", "tensor_tensor", self.dT[:, o, 0:T], self.dT[:, o, 0:T], pd[:, 0:T], op=ALU.add,
                         reads=[b_pd, self.b_dT[o]], writes=[self.b_dT[o]])

    def load_x_tile(self, xrows, T):
        P = self.P
        NB = T // 128
        stage = self.dT.rearrange("p k t -> p (k t)")
        P.dma("sp", stage[:, 0:NB * 2048].rearrange("p (nb f) -> p nb f", f=2048),
              xrows.rearrange("(nb p) f -> p nb f", p=128), writes=self.b_dT)
        for k in range(KC):
            ps, b_p = self.next_ps()
            for nb in range(NB):
                f = P.op if nb == NB - 1 else P.op_nosig
                f("pe", "transpose", ps[:, nb * 128:(nb + 1) * 128],
                  stage[:, nb * 2048 + k * 128: nb * 2048 + (k + 1) * 128], self.ident32,
                  reads=self.b_dT + [self.b_const], writes=[b_p])
            self.copy(self.evac_eng(), self.xT[:, k, 0:T], ps[:, 0:T], [b_p], [self.b_xT[k]])

    def store_y_tile(self, yrows, T):
        P = self.P
        NB = T // 128
        stage = self.dT.rearrange("p k t -> p (k t)")
        for nb in range(NB):
            for g in range(4):
                ps, b_p = self.next_ps()
                for q in range(4):
                    k = g * 4 + q
                    f = P.op if q == 3 else P.op_nosig
                    f("pe", "transpose", ps[:, q * 128:(q + 1) * 128], self.xT[:, k, nb * 128:(nb + 1) * 128], self.ident32,
                      reads=[self.b_xT[k], self.b_const], writes=[b_p])
                self.copy(self.evac_eng(), stage[:, nb * 2048 + g * 512: nb * 2048 + (g + 1) * 512], ps[:, 0:512],
                          [b_p], self.b_dT)
        P.dma("pool", yrows.rearrange("(nb p) f -> p nb f", p=128),
              stage[:, 0:NB * 2048].rearrange("p (nb f) -> p nb f", f=2048), reads=self.b_dT, writes=[Buf("y")])

    def sb_ps(self):
        i = (0, 1, 2, 3, 6, 7)[self.sbps_i % 6]
        self.sbps_i += 1
        return self.ps[i], self.b_ps[i]

    def sb32(self):
        i = self.sb32_i % len(self.sb32_list)
        self.sb32_i += 1
        return self.sb32_list[i]

    def sb16(self):
        i = self.sb16_i % len(self.sb16_list)
        self.sb16_i += 1
        return self.sb16_list[i]

    def sb_attend(self, qT, b_q, Tq, blocks, acc, b_acc, out_ap, out_bufs):
        P = self.P
        C_ = [self.b_const]
        carry, b_c = self.dT[:, 0, 0:Tq], [self.b_dT[0]]
        nblk = len(blocks)
        nk = 128
        st = [dict() for _ in range(nblk)]

        def stage_a(bi):
            d = st[bi]
            blk = blocks[bi]
            if blk[0] == "scr":
                _, h, tj, r, mask = blk
                if r == 3:
                    si = self.kv_i % 3
                    self.kv_i += 1
                    kslot, b_ks = self.bigc(32 + si)
                    vslot, b_vs = self.bigc(35 + si)
                    P.dma("sp", kslot, self.kT_scr[h][:, tj * 512:(tj + 1) * 512], reads=[self.b_kscr[h][tj]], writes=b_ks)
                    P.dma("sp", vslot.rearrange("p (j d) -> p j d", d=128), self.v_scr[h][:, tj * 4:(tj + 1) * 4, :],
                          reads=[self.b_vscr[h][tj]], writes=b_vs)
                    self.cur_kv = (kslot, b_ks, vslot, b_vs)
                kslot, b_ks, vslot, b_vs = self.cur_kv
                kT, bk, v, bv = kslot[:, r * 128:(r + 1) * 128], b_ks, vslot[:, r * 128:(r + 1) * 128], b_vs
            else:
                _, kT, bk, v, bv, mask = blk
            d["v"], d["bv"], d["mask"] = v, bv, mask
            pz, b_pz = self.sb_ps()
            self.mm(pz[0:nk, 0:Tq], b_pz, [(kT, qT, bk + b_q)])
            e32, b_e = self.sb32()
            self.act(e32[0:nk, 0:Tq], pz[0:nk, 0:Tq], AF.Exp, reads=[b_pz], writes=[b_e])
            sp, b_sp = self.sb16()
            self.act(sp[0:nk, 0:Tq], e32[0:nk, 0:Tq], AF.Ln, reads=[b_e], writes=[b_sp], bias=1.0)
            if mask is not None:
                P.op("pool", "tensor_tensor", sp[0:nk, 0:Tq], sp[0:nk, 0:Tq], mask, op=ALU.mult,
                     reads=[b_sp] + C_, writes=[b_sp])
            d["e"], d["be"], d["sp"], d["bsp"] = e32, b_e, sp, b_sp

        def stage_b(bi):
            d = st[bi]
            first, last = bi == 0, bi == nblk - 1
            sp, b_sp = d["sp"], d["bsp"]
            pR, b_pR = self.sb_ps()
            self.mm(pR[0:nk, 0:Tq], b_pR, [(self.negL[0:nk, 0:nk], sp[0:nk, 0:Tq], [b_sp] + C_)])
            if not last:
                pC, b_pC = self.sb_ps()
                self.mm(pC[:, 0:Tq], b_pC, [(self.negones[0:nk, :], sp[0:nk, 0:Tq], [b_sp] + C_)])
            w32, b_w = self.sb32()
            if first:
                self.act(w32[0:nk, 0:Tq], pR[0:nk, 0:Tq], AF.Exp, reads=[b_pR], writes=[b_w])
            else:
                P.op("dve", "tensor_tensor", w32[0:nk, 0:Tq], pR[0:nk, 0:Tq], carry[0:nk, :], op=ALU.add,
                     reads=[b_pR] + b_c, writes=[b_w])
                self.act(w32[0:nk, 0:Tq], w32[0:nk, 0:Tq], AF.Exp, reads=[b_w], writes=[b_w])
            if not last:
                if first:
                    self.copy("dve", carry, pC[:, 0:Tq], [b_pC], b_c)
                else:
                    P.op("dve", "tensor_tensor", carry, carry, pC[:, 0:Tq], op=ALU.add, reads=[b_pC] + b_c, writes=b_c)
            d["w"], d["bw"] = w32, b_w

        def stage_c(bi):
            d = st[bi]
            a16, b_a = self.sb16()
            P.op("pool", "tensor_tensor", a16[0:nk, 0:Tq], d["e"][0:nk, 0:Tq], d["w"][0:nk, 0:Tq], op=ALU.mult,
                 reads=[d["be"], d["bw"]], writes=[b_a])
            if d["mask"] is not None:
                P.op("pool", "tensor_tensor", a16[0:nk, 0:Tq], a16[0:nk, 0:Tq], d["mask"], op=ALU.mult,
                     reads=[b_a] + C_, writes=[b_a])
            d["a"], d["ba"] = a16, b_a

        def stage_d(bi):
            d = st[bi]
            first, last = bi == 0, bi == nblk - 1
            P.op("pe", "matmul", acc, d["v"], d["a"][0:nk, 0:Tq], start=first, stop=last, reads=d["bv"] + [d["ba"]], writes=[b_acc])
            st[bi] = None

        for it in range(nblk + 3):
            if it < nblk:
                stage_a(it)
            if 0 <= it - 3 < nblk:
                stage_d(it - 3)
            if 0 <= it - 2 < nblk:
                stage_c(it - 2)
            if 0 <= it - 1 < nblk:
                stage_b(it - 1)
        self.copy("act", out_ap, acc, [b_acc], out_bufs)

    def sb_stage_prompt(self, ti, T, skp, svp, light=False, orow=0):
        P = self.P
        NB = T // 128
        W = self.W_in
        for h in range(8):
            if True:
                pk, b_pk = self.proj_fm(W, 8 + h, T)
                k16, b_k = self.t16()
                self.copy("act", k16[:, 0:T], pk[:, 0:T], [b_pk], [b_k])
                if True:
                    P.dma("pool", self.kT_scr[h][:, ti * 512: ti * 512 + T], k16[:, 0:T], reads=[b_k], writes=[self.b_kscr[h][ti]])
            for which, fbase, outd in (("k", 8, skp), ("v", 16, svp)):
                if light and which == "k":
                    continue
                pt, b_pt = self.proj_tm1(W, fbase + h, T)
                stg, b_s = self.t32()
                self.copy("dve", stg[:, 0:T], pt[:, 0:T], [b_pt], [b_s])
                if not light:
                    P.dma("pool", outd[orow: orow + T, h * 128:(h + 1) * 128].rearrange("(nb p) d -> p nb d", p=128),
                          stg[:, 0:T].rearrange("p (nb d) -> p nb d", d=128), reads=[b_s], writes=[Buf("o")])
                if which == "v":
                    v16, b_v = self.t16()
                    self.copy("act", v16[:, 0:T], stg[:, 0:T], [b_s], [b_v])
                    P.dma("pool", self.v_scr[h][:, ti * 4: ti * 4 + NB, :], v16[:, 0:T].rearrange("p (nb d) -> p nb d", d=128),
                          reads=[b_v], writes=[self.b_vscr[h][ti]])
        if light:
            return
        self.sb32_list = [(self.dT[:, k, :], self.b_dT[k]) for k in range(1, KC)]
        self.sb16_list = [(self.big[:, c, :], self.b_big[c]) for c in list(range(24, 32)) + [38, 39, 40, 41, 43]]
        for h in range(8):
            pq, b_pq = self.proj_fm(W, h, T)
            qT, b_q = self.bigc(42, T)
            P.op("act", "mul", qT, pq[:, 0:T], 128.0 ** -0.5, reads=[b_pq], writes=b_q)
            blocks = []
            for tj in range(ti, -1, -1):
                for r in range(3, -1, -1):
                    blocks.append(("scr", h, tj, r, self.sbmask[:, r, 0:T] if tj == ti else None))
            acc, b_acc = self.ps[4 + h % 2][:, 0:T], self.b_ps[4 + h % 2]
            o, b_o = self.bigc(h, T)
            self.sb_attend(qT, b_q, T, blocks, acc, b_acc, o, b_o)


    def gla_stage(self, T, sample, sgla=None, glas=None, light=False):
        P = self.P
        NB = T // 128
        W = self.W_in
        C = [self.b_const]
        triN, U, mLE = (self.tri_s, self.U_s, self.mLE_s) if sample else (self.tri_p, self.U_p, self.mLE_p)
        ps, b_p = self.next_ps()
        self.mm(ps[0:16, 0:T], b_p, [(self.wga_sb[:, k, :], self.nT[:, k, 0:T], C + [self.b_nT[k]]) for k in range(KC)])
        gaT, b_ga = self.big32(18)
        self.copy("act", gaT[0:16, 0:T], ps[0:16, 0:T], [b_p], b_ga)
        for nb in range(NB):
            for half in range(2):
                ps, b_p = self.next_ps()
                self.mm(ps[:, 0:512], b_p, [(gaT[0:16, nb * 128:(nb + 1) * 128], self.wa2_sb[0:16, half * 512:(half + 1) * 512], b_ga + C)])
                l, b_l = self.big32(19 + half)
                P.op("dve", "tensor_tensor", l, ps[:, 0:512], self.ba2_bc[:, half * 512:(half + 1) * 512], op=ALU.add,
                     reads=[b_p] + C, writes=b_l)
                self.act(l, l, AF.Exp, reads=b_l, writes=b_l, scale=-1.0)
                self.act(l, l, AF.Ln, reads=b_l, writes=b_l, bias=1.0)
                ps, b_p = self.next_ps()
                for q in range(4):
                    f = P.op if q == 3 else P.op_nosig
                    f("pe", "matmul", ps[:, q * 128:(q + 1) * 128], l[:, q * 128:(q + 1) * 128], triN, start=True, stop=True,
                      reads=b_l + C, writes=[b_p])
                P.op("dve", "tensor_copy", self.dT[:, half * 4:half * 4 + 4, nb * 128:(nb + 1) * 128],
                     ps[:, 0:512].rearrange("p (q t) -> p q t", t=128), reads=[b_p],
                     writes=[self.b_dT[half * 4 + q] for q in range(4)])
                ps2, b_p2 = self.next_ps()
                self.mm(ps2[:, 0:512], b_p2, [(U, l, b_l + C)])
                self.act(self.dT[:, 8 + nb * 2 + half, :], ps2[:, 0:512], AF.Exp, reads=[b_p2], writes=[self.b_dT[8 + nb * 2 + half]])
        for half in range(2):
            banks = self.proj_tm4(W, [32 + half * 4 + q for q in range(4)], T)
            for nb in range(NB):
                ps, b_p = banks[nb]
                P.op("dve", "tensor_tensor", self.big[:, nb * 2 + half, :], ps[:, 0:512], self.dT[:, 8 + nb * 2 + half, :], op=ALU.mult,
                     reads=[b_p, self.b_dT[8 + nb * 2 + half]], writes=[self.b_big[nb * 2 + half]])
        S16 = self.big[:, 33:35, :]
        b_S16 = [self.b_big[33], self.b_big[34]]
        attT, b_att = self.big[:, 32, 0:128], [self.b_big[32]]
        for h in range(4):
            banks = self.proj_tm4(W, [40 + 4 * h + q for q in range(4)], T)
            for nb in range(NB):
                ps, b_p = banks[nb]
                self.copy(self.evac_eng(), self.big[:, 24 + nb, :], ps[:, 0:512], [b_p], [self.b_big[24 + nb]])
            for i in range(2):
                c = 2 * h + i
                if sample:
                    self.act(self.eblast[:, i, 0:4], self.dT[:, c, 31:128:32], AF.Exp, reads=[self.b_dT[c]], writes=[self.b_ebl])
                else:
                    self.act(self.eblast[:, i, 0:NB], self.dT[:, c, 127:T:128], AF.Exp, reads=[self.b_dT[c]], writes=[self.b_ebl])
                if light:
                    continue
                pq, b_pq = self.proj_fm(W, 24 + c, T)
                eb, b_eb = self.t32()
                self.act(eb[:, 0:T], self.dT[:, c, 0:T], AF.Exp, reads=[self.b_dT[c]], writes=[b_eb])
                P.op("dve", "scalar_tensor_tensor", self.big[:, 28 + i, 0:T], pq[:, 0:T], 256.0 ** -0.5, eb[:, 0:T],
                     op0=ALU.mult, op1=ALU.mult, reads=[b_pq, b_eb], writes=[self.b_big[28 + i]])
                pk, b_pk = self.proj_fm(W, 32 + c, T)
                enb, b_enb = self.t32()
                self.act(enb[:, 0:T], self.dT[:, c, 0:T], AF.Exp, reads=[self.b_dT[c]], writes=[b_enb], scale=-1.0)
                P.op("dve", "tensor_tensor", self.big[:, 30 + i, 0:T], pk[:, 0:T], enb[:, 0:T], op=ALU.mult,
                     reads=[b_pk, b_enb], writes=[self.b_big[30 + i]])
            bS = [self.b_S32[2 * h], self.b_S32[2 * h + 1]]
            for ch in range(NB):
                cs = slice(ch * 128, (ch + 1) * 128)
                if not light:
                    ps, b_p = self.next_ps()
                    self.mm(ps[:, 0:128], b_p, [(self.big[:, 30 + i, cs], self.big[:, 28 + i, cs], [self.b_big[30 + i], self.b_big[28 + i]])
                                                for i in range(2)])
                    P.op("dve", "tensor_tensor", attT, ps[:, 0:128], mLE, op=ALU.mult, reads=[b_p] + C, writes=b_att)
                segs = [(sq * 32, 32, sq) for sq in range(4)] if sample else [(ch * 128, 128, ch)]
                for (c0, n, si) in segs:
                    cq = slice(c0, c0 + n)
                    lq = slice(c0 - ch * 128, c0 - ch * 128 + n)
                    if sample:
                        P.dma("sp", self.S32[:, 2 * h:2 * h + 2, :], sgla[si][2 * h:2 * h + 2].rearrange("j p d -> p j d"), writes=bS)
                    for j in range(2):
                        if not light:
                            self.copy("act", S16[:, j, :], self.S32[:, 2 * h + j, :], [bS[j]], [b_S16[j]])
                    for dvc in range(4):
                        if light:
                            break
                        dv = slice(dvc * 128, (dvc + 1) * 128)
                        terms = [(S16[:, j, dv], self.big[:, 28 + j, cq], [b_S16[j], self.b_big[28 + j]]) for j in range(2)]
                        terms.append((self.big[:, 24 + ch, dv], attT[:, lq], [self.b_big[24 + ch]] + b_att))
                        self.mm(self.ps[4 + dvc][:, cq], self.b_ps[4 + dvc], terms)
                    cell = ch * 2 + (2 * h) // 4
                    off = ((2 * h) % 4) * 128
                    if sample:
                        kd, b_kd = self.big[:, 35, 0:256], [self.b_big[35]]
                        P.op("dve", "tensor_scalar_mul", kd, self.big[:, cell, off:off + 256], self.seqmask[:, si:si + 1],
                             reads=[self.b_big[cell]] + C, writes=b_kd)
                    else:
                        kd, b_kd = self.big[:, cell, off:off + 256], [self.b_big[cell]]
                    for j in range(2):
                        ps, b_p = self.next_ps()
                        self.mm(ps[:, 0:512], b_p, [(kd[:, j * 128:(j + 1) * 128], self.big[:, 24 + ch, :], b_kd + [self.b_big[24 + ch]])])
                        P.op("dve", "scalar_tensor_tensor", self.S32[:, 2 * h + j, :], self.S32[:, 2 * h + j, :],
                             self.eblast[:, j, si:si + 1], ps[:, 0:512], op0=ALU.mult, op1=ALU.add,
                             reads=[bS[j], self.b_ebl, b_p], writes=[bS[j]])
                    if sample:
                        P.dma("pool", glas[si][2 * h:2 * h + 2].rearrange("j p d -> p j d"), self.S32[:, 2 * h:2 * h + 2, :],
                              reads=bS, writes=[Buf("o")])
            if light:
                continue
            rstd, b_r = self.rstd_from([(self.ps[4 + dvc][:, 0:T], [self.b_ps[4 + dvc]]) for dvc in range(4)], T, self.ones512)
            for dvc in range(4):
                pr, b_pr = self.proj_fm(W, 56 + 4 * h + dvc, T)
                sr, b_sr = self.t32()
                self.act(sr[:, 0:T], pr[:, 0:T], AF.Silu, reads=[b_pr], writes=[b_sr])
                tmp, b_t = self.t32()
                P.op("dve", "scalar_tensor_tensor", tmp[:, 0:T], self.ps[4 + dvc][:, 0:T], self.glag_sb[:, dvc:dvc + 1], rstd[:, 0:T],
                     op0=ALU.mult, op1=ALU.mult, reads=[self.b_ps[4 + dvc], b_r] + C, writes=[b_t])
                P.op("pool", "tensor_tensor", self.big[:, 8 + 4 * h + dvc, 0:T], tmp[:, 0:T], sr[:, 0:T], op=ALU.mult,
                     reads=[b_t, b_sr], writes=[self.b_big[8 + 4 * h + dvc]])

    def mem_prologue(self, mem, memk, memv):
        P = self.P
        W = self.W_memkv
        self.load_x_tile(mem, 256)
        self.norm_to_nT(self.G(6), 256)
        for c in range(8):
            pk, b_pk = self.proj_fm(W, c, 256)
            self.copy(self.evac_eng(), self.mkT16[:, c, :], pk[:, 0:256], [b_pk], [self.b_mk])
        for grp, outd in ((0, memk), (1, memv)):
            for half in range(2):
                banks = self.proj_tm4(W, [grp * 8 + half * 4 + q for q in range(4)], 256)
                for nb in range(2):
                    ps, b_p = banks[nb]
                    stg, b_s = self.t32()
                    self.copy("act", stg, ps[:, 0:512], [b_p], [b_s])
                    P.dma("pool", outd[nb * 128:(nb + 1) * 128, half * 512:(half + 1) * 512], stg, reads=[b_s], writes=[Buf("o")])
                    if grp == 1:
                        self.copy("dve", self.mv16[:, nb, half * 512:(half + 1) * 512], stg, [b_s], [self.b_mv])

    def mem_stage(self, T, sample, cmk=None, cmv=None):
        P = self.P
        W = self.W_in
        C = [self.b_const]
        for c in range(8):
            pq, b_pq = self.proj_fm(W, 72 + c, T)
            P.op("act", "mul", self.big[:, 32 + c, 0:T], pq[:, 0:T], 1.0 / 16.0, reads=[b_pq], writes=[self.b_big[32 + c]])
        segs = [(sq * 32, 32, sq) for sq in range(4)] if sample else [(0, T, None)]
        for (c0, n, sq) in segs:
            if sample:
                stage = self.dT[:, 0:4, :].rearrange("p (b k) t -> p b (k t)", b=2)
                bst = self.b_dT[0:4]
                P.dma("sp", stage, cmk[sq].rearrange("(b p) f -> p b f", p=128), writes=bst)
                for c in range(8):
                    ps, b_p = self.next_ps()
                    for mb in range(2):
                        f = P.op if mb == 1 else P.op_nosig
                        f("pe", "transpose", ps[:, mb * 128:(mb + 1) * 128], stage[:, mb, c * 128:(c + 1) * 128], self.ident32,
                          reads=bst + C, writes=[b_p])
                    self.copy(self.evac_eng(), self.mkT16[:, c, :], ps[:, 0:256], [b_p], [self.b_mk])
                P.dma("pool", self.mv16, cmv[sq].rearrange("(b p) f -> p b f", p=128), writes=[self.b_mv])
            for h in range(4):
                p16 = []
                for mb in range(2):
                    ps, b_p = self.next_ps()
                    self.mm(ps[:, 0:n], b_p, [(self.mkT16[:, 2 * h + i, mb * 128:(mb + 1) * 128], self.big[:, 32 + 2 * h + i, c0:c0 + n],
                                               [self.b_mk, self.b_big[32 + 2 * h + i]]) for i in range(2)])
                    pt, b_pt = self.t16()
                    self.act(pt[:, 0:n], ps[:, 0:n], AF.Exp, reads=[b_p], writes=[b_pt])
                    p16.append((pt, b_pt))
                ps, b_p = self.next_ps()
                self.mm(ps[:, 0:n], b_p, [(self.onesM, pt[:, 0:n], C + [b_pt]) for (pt, b_pt) in p16])
                rden, b_rd = self.t32()
                P.op("dve", "reciprocal", rden[:, 0:n], ps[:, 0:n], reads=[b_p], writes=[b_rd])
                for i in range(2):
                    ps, b_p = self.next_ps()
                    self.mm(ps[:, 0:n], b_p, [(self.mv16[:, mb, (2 * h + i) * 128:(2 * h + i + 1) * 128], p16[mb][0][:, 0:n],
                                               [self.b_mv, p16[mb][1]]) for mb in range(2)])
                    P.op("dve", "tensor_tensor", self.big[:, 24 + 2 * h + i, c0:c0 + n], ps[:, 0:n], rden[:, 0:n], op=ALU.mult,
                         reads=[b_p, b_rd], writes=[self.b_big[24 + 2 * h + i]])

    def merge_stage(self, T):
        P = self.P
        C = [self.b_const]
        branches = [(self.W_sbbr, 0), (self.W_glabr, 8), (self.W_membr, 24)]
        macc, b_m = self.dT[:, 8, 0:T], [self.b_dT[8]]
        for j in range(KC):
            for i, (Wb, cell0) in enumerate(branches):
                pg, b_pg = self.proj_fm(self.W_gate, i * 16 + j, T)
                g, b_g = self.t32()
                self.act(g[:, 0:T], pg[:, 0:T], AF.Sigmoid, reads=[b_pg] + C, writes=[b_g], bias=self.bgate_sb[:, i * 16 + j:i * 16 + j + 1])
                pb, b_pb = self.proj_fm(Wb, j, T, src=self.big, b_src=self.b_big, src_k0=cell0)
                if i == 0:
                    P.op("dve", "tensor_tensor", macc, g[:, 0:T], pb[:, 0:T], op=ALU.mult, reads=[b_g, b_pb], writes=b_m)
                else:
                    t, b_t = self.t32()
                    P.op("dve", "tensor_tensor", t[:, 0:T], g[:, 0:T], pb[:, 0:T], op=ALU.mult, reads=[b_g, b_pb], writes=[b_t])
                    if i == 1:
                        P.op("pool", "tensor_tensor", macc, macc, t[:, 0:T], op=ALU.add, reads=b_m + [b_t], writes=b_m)
                    else:
                        o, b_o = self.dT16(j, T)
                        P.op("pool", "tensor_tensor", o, macc, t[:, 0:T], op=ALU.add, reads=b_m + [b_t], writes=b_o)
        b16 = [self.b_dT[c // 2] for c in range(KC)]
        outs = []
        for o in range(KC):
            pm, b_pm = self.proj_fm(self.W_out, o, T, src=self.dT16v, b_src=b16)
            dst, b_d = self.big32(o, T)
            self.copy(self.evac_eng(), dst, pm[:, 0:T], [b_pm], b_d)
            outs.append((dst, b_d))
        self.postnorm_add(outs, self.G(3), T)

    def sb_stage_sample(self, sks, svs, csk, csv):
        P = self.P
        T = 128
        W = self.W_in
        C = [self.b_const]
        self.sb32_list = [(self.dT[:, k, :], self.b_dT[k]) for k in range(4, KC)]
        self.sb16_list = [(self.big[:, c, :], self.b_big[c]) for c in list(range(24, 32)) + [40, 41, 43]]
        for h in range(8):
            pk, b_pk = self.proj_fm(W, 8 + h, T)
            kTn, b_kn = self.bigc(38, T)
            self.copy("act", kTn, pk[:, 0:T], [b_pk], b_kn)
            pt, b_pt = self.proj_tm1(W, 8 + h, T)
            stg, b_s = self.t32()
            self.copy("dve", stg[:, 0:128], pt[:, 0:128], [b_pt], [b_s])
            P.dma("pool", sks[:, h * 128:(h + 1) * 128], stg[:, 0:128], reads=[b_s], writes=[Buf("o")])
            pv, b_pv = self.proj_tm1(W, 16 + h, T)
            stg, b_s = self.t32()
            self.copy("dve", stg[:, 0:128], pv[:, 0:128], [b_pv], [b_s])
            P.dma("pool", svs[:, h * 128:(h + 1) * 128], stg[:, 0:128], reads=[b_s], writes=[Buf("o")])
            vN, b_vn = self.bigc(39, 128)
            self.copy("act", vN, stg[:, 0:128], [b_s], b_vn)
            pq, b_pq = self.proj_fm(W, h, T)
            qT, b_q = self.bigc(42, T)
            P.op("act", "mul", qT, pq[:, 0:T], 128.0 ** -0.5, reads=[b_pq], writes=b_q)
            for sq in range(4):
                stage = self.dT[:, 2:4, :].rearrange("p a (b d) -> p (a b) d", d=128)
                bst = self.b_dT[2:4]
                P.dma("sp", stage, csk[sq][:, h * 128:(h + 1) * 128].rearrange("(kb p) d -> p kb d", p=128), writes=bst)
                for g in range(2):
                    ps, b_p = self.next_ps()
                    for q in range(4):
                        f = P.op if q == 3 else P.op_nosig
                        f("pe", "transpose", ps[:, q * 128:(q + 1) * 128], stage[:, g * 4 + q, :], self.ident32, reads=bst + C, writes=[b_p])
                    self.copy(self.evac_eng(), self.big[:, 32 + g, :], ps[:, 0:512], [b_p], [self.b_big[32 + g]])
                vP = self.big[:, 35:37, :].rearrange("p a (b d) -> p (a b) d", d=128)
                bvp = self.b_big[35:37]
                P.dma("pool", vP, csv[sq][:, h * 128:(h + 1) * 128].rearrange("(kb p) d -> p kb d", p=128), writes=bvp)
                blocks = [("sb", kTn, b_kn, vN, b_vn, self.sbmask_s[:, sq, :])]
                for kb in range(7, -1, -1):
                    blocks.append(("sb", self.big[:, 32 + kb // 4, (kb % 4) * 128:(kb % 4 + 1) * 128], [self.b_big[32 + kb // 4]],
                                   vP[:, kb, :], bvp, None))
                ai = 4 + (h * 4 + sq) % 2
                self.sb_attend(qT[:, sq * 32:(sq + 1) * 32], b_q, 32, blocks, self.ps[ai][:, 0:32], self.b_ps[ai],
                               self.big[:, h, sq * 32:(sq + 1) * 32], [self.b_big[h]])

    def mask_sel(self, ap, cm, coef, n, base, op):
        self.P.op("pool", "affine_select", out=ap, in_=ap, compare_op=op, fill=0.0, base=base, pattern=[[coef, n]],
                  channel_multiplier=cm, reads=[self.b_const], writes=[self.b_const])

    def build(self):
        nc = bass.Bass("TRN2", target_bir_lowering=False)
        self.nc = nc
        self.P = Prog(nc)
        P = self.P
        SEQ, PAST, NS = self.SEQ, self.PAST, self.NS
        C = None
        with contextlib.ExitStack() as st:
            self.st = st
            OWN = SEQ // 2
            xp = self.din("xp", [OWN, D]); xpre = self.din("xpre", [OWN, D]); xs = self.din("xs", [128, D]); mem = self.din("mem", [256, D])
            csk = self.din("csk", [NS, PAST, 1024]); csv = self.din("csv", [NS, PAST, 1024])
            sgla = self.din("sgla", [NS, 8, 128, 512])
            cmk = self.din("cmk", [NS, 256, 1024]); cmv = self.din("cmv", [NS, 256, 1024])
            w_gu1 = self.din("w_gu1", [D, 2 * DFF]); w_d1 = self.din("w_d1", [DFF, D])
            w_gu2 = self.din("w_gu2", [D, 2 * DFF]); w_d2 = self.din("w_d2", [DFF, D])
            w_in = self.din("w_in", [D, 10256])
            w_a2 = self.din("w_a2", [16, 1024]); ba2 = self.din("ba2", [128, 1024])
            w_memkv = self.din("w_memkv", [D, 2048])
            w_sbbr = self.din("w_sbbr", [1024, D]); w_glabr = self.din("w_glabr", [2048, D]); w_membr = self.din("w_membr", [1024, D])
            w_gate = self.din("w_gate", [D, 3 * D]); w_out = self.din("w_out", [D, D])
            gains = self.din("gains", [128, 7 * 16]); bgate = self.din("bgate", [128, 48]); glag = self.din("glag", [128, 4])
            yp = self.dout("yp", [OWN, D]); ys = self.dout("ys", [128, D])
            skp = self.dout("skp", [OWN, 1024]); svp = self.dout("svp", [OWN, 1024])
            glap = self.dout("glap", [8, 128, 512])
            memk = self.dout("memk", [256, 1024]); memv = self.dout("memv", [256, 1024])
            sks = self.dout("sks", [128, 1024]); svs = self.dout("svs", [128, 1024])
            glas = self.dout("glas", [NS, 8, 128, 512])
            self.kT_scr = self.dscr("kT_scr", [8, 128, SEQ])
            self.v_scr = self.dscr("v_scr", [8, 128, SEQ // 128, 128])
            self.b_kscr = [[Buf("kscr") for _ in range(self.NT)] for _ in range(8)]
            self.b_vscr = [[Buf("vscr") for _ in range(self.NT)] for _ in range(8)]
            self.xT = self.sb("xT", [128, KC, 512]); self.b_xT = [Buf("xT%d" % k) for k in range(KC)]
            self.nT = self.sb("nT", [128, KC, 512], BF16); self.b_nT = [Buf("nT%d" % k) for k in range(KC)]
            self.big = self.sb("big", [128, NJ, 512], BF16); self.b_big = [Buf("big%d" % k) for k in range(NJ)]
            self.big32v = self.big.rearrange("p c t -> p (c t)").bitcast(F32).rearrange("p (c t) -> p c t", t=512)
            self.dT = self.sb("dT", [128, KC, 512]); self.b_dT = [Buf("dT%d" % k) for k in range(KC)]
            self.dT16v = self.dT.rearrange("p c t -> p (c t)").bitcast(BF16).rearrange("p (c t) -> p c t", t=512)
            NWS = 4
            self.ws = self.sb("ws", [128, NWS, 22 * 128], BF16); self.b_ws = [Buf("ws%d" % k) for k in range(NWS)]
            self.r32 = self.sb("r32", [128, 4, 512]); self.b_r32 = [Buf("r32_%d" % k) for k in range(4)]
            self.r16 = self.sb("r16", [128, 4, 512], BF16); self.b_r16 = [Buf("r16_%d" % k) for k in range(4)]
            self.rstd = self.sb("rstd", [128, 512]); self.b_rstd = Buf("rstd")
            self.S32 = self.sb("S32", [128, 8, 512]); self.b_S32 = [Buf("S32_%d" % k) for k in range(8)]
            self.mkT16 = self.sb("mkT16", [128, 8, 256], BF16); self.b_mk = Buf("mk")
            self.mv16 = self.sb("mv16", [128, 2, 1024], BF16); self.b_mv = Buf("mv")
            self.eblast = self.sb("eblast", [128, 2, 4]); self.b_ebl = Buf("ebl")
            self.ident32 = self.sb("ident32", [128, 128])
            self.onesD = self.sb("onesD", [128, 128], BF16)
            self.ones512 = self.sb("ones512", [128, 128], BF16)
            self.onesM = self.sb("onesM", [128, 128], BF16)
            self.negones = self.sb("negones", [128, 128], BF16)
            self.negL = self.sb("negL", [128, 128], BF16)
            self.sbmask = self.sb("sbmask", [128, 4, 512], BF16)
            self.sbmask_s = self.sb("sbmask_s", [128, 4, 32], BF16)
            self.seqmask = self.sb("seqmask", [128, 4])
            self.tri_p = self.sb("tri_p", [128, 128]); self.U_p = self.sb("U_p", [128, 128]); self.mLE_p = self.sb("mLE_p", [128, 128])
            self.tri_s = self.sb("tri_s", [128, 128]); self.U_s = self.sb("U_s", [128, 128]); self.mLE_s = self.sb("mLE_s", [128, 128])
            self.gcols = self.sb("gcols", [128, 7 * 16])
            self.bgate_sb = self.sb("bgate_sb", [128, 48])
            self.glag_sb = self.sb("glag_sb", [128, 4])
            self.ba2_bc = self.sb("ba2_bc", [128, 1024])
            self.wa2_sb = self.sb("wa2_sb", [16, 1024])
            self.wga_sb = self.sb("wga_sb", [128, 16, 16], BF16)
            self.b_const = Buf("const")
            C = [self.b_const]
            self.ps = [st.enter_context(nc.psum_tensor("ps%d" % i, [128, 512], F32))[:] for i in range(8)]
            self.b_ps = [Buf("ps%d" % i) for i in range(8)]
            self.ps_i = self.r32_i = self.r16_i = self.ws_i = self.ev_i = self.kv_i = self.sbps_i = self.sb32_i = self.sb16_i = 0
            self.G = lambda i: self.gcols[:, i * 16:(i + 1) * 16]

            def ms(ap, v):
                P.op("pool", "memset", ap, v, writes=C)
            GE, GT = ALU.is_ge, ALU.is_gt
            ms(self.ident32, 0.0)
            P.op("pool", "affine_select", out=self.ident32, in_=self.ident32, compare_op=ALU.not_equal, fill=1.0, base=0,
                 pattern=[[-1, 128]], channel_multiplier=1, reads=C, writes=C)
            ms(self.onesD, 1.0 / D); ms(self.ones512, 1.0 / 512); ms(self.onesM, 1.0); ms(self.negones, -1.0)
            ms(self.negL, -1.0); self.mask_sel(self.negL, 1, -1, 128, 0, GE)
            for r in range(4):
                ms(self.sbmask[:, r, :], 1.0); self.mask_sel(self.sbmask[:, r, :], -1, 1, 512, -128 * r, GT)
            for sq in range(4):
                ms(self.sbmask_s[:, sq, :], 1.0)
                self.mask_sel(self.sbmask_s[:, sq, :], 1, 0, 32, -32 * sq, GE)
                self.mask_sel(self.sbmask_s[:, sq, :], -1, 1, 32, 32 * sq, GT)
                ms(self.seqmask[:, sq:sq + 1], 1.0)
                self.mask_sel(self.seqmask[:, sq:sq + 1], 1, 0, 1, -32 * sq, GE)
                self.mask_sel(self.seqmask[:, sq:sq + 1], -1, 0, 1, 32 * sq + 31, GE)
            for (tri, U, mLE, blk) in ((self.tri_p, self.U_p, self.mLE_p, False), (self.tri_s, self.U_s, self.mLE_s, True)):
                ms(tri, -1.0 / 16); self.mask_sel(tri, -1, 1, 128, 0, GE)
                ms(U, -1.0 / 16); self.mask_sel(U, 1, -1, 128, 0, GT)
                ms(mLE, 1.0); self.mask_sel(mLE, -1, 1, 128, 0, GE)
                if blk:
                    for m in (tri, U, mLE):
                        for bt in range(4):
                            sl = m[:, bt * 32:(bt + 1) * 32]
                            self.mask_sel(sl, 1, 0, 32, -32 * bt, GE)
                            self.mask_sel(sl, -1, 0, 32, 32 * bt + 31, GE)
            P.dma("sp", self.gcols, gains, writes=C)
            P.dma("sp", self.bgate_sb, bgate, writes=C)
            P.dma("sp", self.glag_sb, glag, writes=C)
            P.dma("sp", self.ba2_bc, ba2, writes=C)
            P.dma("sp", self.wa2_sb, w_a2, writes=C)
            for gi in (1, 5):
                P.op("act", "mul", self.G(gi), self.G(gi), 0.5, reads=C, writes=C)
            for k in range(8):
                P.op("pool", "memset", self.S32[:, k, :], 0.0, writes=[self.b_S32[k]])

            self.conv_pending = []
            c128 = lambda n: [128 * i for i in range(n)]
            self.W_memkv = self.conv_weight("memkv", w_memkv, D, c128(16))
            self.W_gu1 = self.conv_weight("gu1", w_gu1, D, c128(88))
            self.W_d1 = self.conv_weight("d1", w_d1, DFF, c128(16))
            self.flush_conv()
            wga_scr = self.dscr("wga_scr", [128, 16, 16])
            b_wga = Buf("wga")
            P.dma("pool", wga_scr, w_in[:, 9216:9232].rearrange("(kc p) c -> p kc c", p=128), writes=[b_wga])
            P.dma("sp", self.wga_sb, wga_scr, reads=[b_wga], writes=C)
            self.W_in = self.conv_weight("win", w_in, D, c128(72) + [9232 + 128 * i for i in range(8)])
            self.mem_prologue(mem, memk, memv)
            self.flush_conv()
            self.W_gate = self.conv_weight("gate", w_gate, D, c128(48))
            self.W_sbbr = self.conv_weight("sbbr", w_sbbr, 1024, c128(16))
            self.W_glabr = self.conv_weight("glabr", w_glabr, 2048, c128(16))
            self.W_membr = self.conv_weight("membr", w_membr, 1024, c128(16))
            self.W_out = self.conv_weight("out", w_out, D, c128(16))
            self.W_gu2 = self.conv_weight("gu2", w_gu2, D, c128(88))
            self.W_d2 = self.conv_weight("d2", w_d2, DFF, c128(16))

            NH = self.NT // 2
            per_tile = (len(self.conv_pending) + max(NH - 1, 1) - 1) // max(NH - 1, 1)
            tiles = [(i, xpre[i * 512:(i + 1) * 512, :], None, 512, False, True, 0) for i in range(NH)]
            tiles += [(NH + i, xp[i * 512:(i + 1) * 512, :], yp[i * 512:(i + 1) * 512, :], 512, False, False, i * 512) for i in range(NH)]
            tiles.append((self.NT, xs, ys, 128, True, False, 0))
            for (ti, xin, yout, T, sample, light, orow) in tiles:
                self.load_x_tile(xin, T)
                self.norm_to_nT(self.G(0), T)
                self.ffn(self.W_gu1, self.W_d1, T)
                self.postnorm_add([(self.dT[:, k, 0:T], [self.b_dT[k]]) for k in range(KC)], self.G(1), T)
                self.norm_to_nT(self.G(2), T)
                self.gla_stage(T, sample, sgla, glas, light=light)
                if sample:
                    self.sb_stage_sample(sks, svs, csk, csv)
                else:
                    self.sb_stage_prompt(ti, T, skp, svp, light=light, orow=orow)
                if light:
                    self.flush_conv(per_tile)
                    continue
                self.flush_conv()
                self.mem_stage(T, sample, cmk, cmv)
                self.merge_stage(T)
                if ti == self.NT - 1:
                    P.dma("pool", glap.rearrange("c p d -> p c d"), self.S32, reads=self.b_S32, writes=[Buf("o")])
                self.norm_to_nT(self.G(4), T)
                self.ffn(self.W_gu2, self.W_d2, T)
                self.postnorm_add([(self.dT[:, k, 0:T], [self.b_dT[k]]) for k in range(KC)], self.G(5), T)
                self.store_y_tile(yout, T)
            P.finish()
            P.emit()
        return nc


def _colmat(v, n):
    return np.ascontiguousarray(np.asarray(v, np.float32).reshape(n, 128).T)


def kernel(**inp):
    f = lambda k: np.asarray(inp[k], np.float32)
    x_prompt, x_sample, mem_prompt = f("x_prompt"), f("x_sample"), f("mem_prompt")
    B, SEQ, _ = x_prompt.shape
    csk, csv, sg = f("cache_sb_k")[0], f("cache_sb_v")[0], f("state_gla")[0]
    cmk, cmv = f("cache_mem_k")[0], f("cache_mem_v")[0]
    PAST = csk.shape[1]
    gains = np.concatenate([_colmat(f(k)[0], 16) for k in
                            ("ffn1_pre_g", "ffn1_post_g", "mix_pre_g", "mix_post_g", "ffn2_pre_g", "ffn2_post_g", "mem_norm_g")], axis=1)
    shared = dict(
        w_gu1=f("ffn1_w_gu")[0], w_d1=f("ffn1_w_d")[0], w_gu2=f("ffn2_w_gu")[0], w_d2=f("ffn2_w_d")[0], w_in=f("w_in")[0],
        w_a2=f("gla_w_a2")[0], ba2=np.ascontiguousarray(np.broadcast_to(f("gla_b_a2")[0][None, :], (128, 1024))),
        w_memkv=f("w_mem_kv")[0], w_sbbr=f("w_sb_br")[0], w_glabr=f("w_gla_br")[0], w_membr=f("w_mem_br")[0],
        w_gate=f("w_gate")[0], w_out=f("w_out")[0], gains=np.ascontiguousarray(gains),
        bgate=_colmat(f("b_gate")[0], 48), glag=_colmat(f("gla_norm_g")[0], 4))
    in_maps = []
    OWN = SEQ // 2
    zeros = np.zeros((OWN, D), np.float32)
    for c in range(8):
        b, half = (c // 2) % B, c % 2
        s = slice(4 * c, 4 * c + 4)
        m = dict(shared)
        m.update(xp=np.ascontiguousarray(x_prompt[b, half * OWN:(half + 1) * OWN]),
                 xpre=(np.ascontiguousarray(x_prompt[b, 0:OWN]) if half else zeros),
                 xs=np.ascontiguousarray(x_sample[s].reshape(128, D)), mem=mem_prompt[b],
                 csk=np.ascontiguousarray(csk[s].reshape(4, PAST, 1024)), csv=np.ascontiguousarray(csv[s].reshape(4, PAST, 1024)),
                 sgla=np.ascontiguousarray(sg[s].reshape(4, 8, 128, 512)),
                 cmk=np.ascontiguousarray(cmk[s].reshape(4, 256, 1024)), cmv=np.ascontiguousarray(cmv[s].reshape(4, 256, 1024)))
        in_maps.append(m)
    nc = Builder(seq=SEQ, past=PAST).build()
    res = run_bass_kernel_spmd(nc, in_maps, core_ids=list(range(8))).results
    cat = lambda k, b: np.concatenate([res[2 * b][k], res[2 * b + 1][k]], 0)
    y_p = np.stack([cat("yp", b) for b in range(B)])
    y_s = np.concatenate([res[c]["ys"].reshape(4, 32, D) for c in range(8)])
    sk_p = np.stack([cat("skp", b).reshape(SEQ, 8, 128) for b in range(B)])[None]
    sv_p = np.stack([cat("svp", b).reshape(SEQ, 8, 128) for b in range(B)])[None]
    gla_p = np.stack([res[2 * b + 1]["glap"].reshape(4, 256, 512) for b in range(B)])[None]
    mk_p = np.stack([res[2 * b]["memk"].reshape(256, 4, 256) for b in range(B)])[None]
    mv_p = np.stack([res[2 * b]["memv"].reshape(256, 4, 256) for b in range(B)])[None]
    sk_s = np.concatenate([res[c]["sks"].reshape(4, 32, 8, 128) for c in range(8)])[None]
    sv_s = np.concatenate([res[c]["svs"].reshape(4, 32, 8, 128) for c in range(8)])[None]
    gla_s = np.concatenate([res[c]["glas"].reshape(4, 4, 256, 512) for c in range(8)])[None]
    return (y_p, y_s, sk_p, sv_p, gla_p, mk_p, mv_p, sk_s, sv_s, gla_s)
```

```python
import contextlib
import numpy as np
import concourse.bass as bass
import concourse.mybir as mybir
from concourse.bass_utils import run_bass_kernel_spmd

F32 = mybir.dt.float32
BF16 = mybir.dt.bfloat16
AF = mybir.ActivationFunctionType
ALU = mybir.AluOpType

D = 2048
KC = 16
DFF = 5632
NJ = 44
EPS = 1e-6
ENGS = ("pe", "act", "dve", "pool", "sp")
NDMA_SEMS = 12


class Buf:
    __slots__ = ("name", "w", "r")

    def __init__(self, name):
        self.name = name
        self.w = None
        self.r = {}


class Prog:
    def __init__(self, nc):
        self.nc = nc
        self.ops = {e: [] for e in ENGS}
        self.cnt = {}
        self.seen = {e: {} for e in ENGS}
        self.nd = {}
        self.eng_keys = {}
        self.dma_keys = []

    def _deps(self, eng, reads, writes):
        need = {}

        def add(tok):
            if tok is None:
                return
            k, c = tok
            if need.get(k, 0) < c:
                need[k] = c

        for b in reads:
            add(b.w)
        for b in writes:
            add(b.w)
            for k, c in b.r.items():
                add((k, c))
        waits = []
        seen = self.seen[eng]
        for k, c in need.items():
            if eng == "pe" and k == "pe":
                continue
            if seen.get(k, 0) < c:
                seen[k] = c
                waits.append((k, c))
        return waits

    def _commit(self, tok, reads, writes):
        k, c = tok
        for b in reads:
            if b.r.get(k, 0) < c:
                b.r[k] = c
        for b in writes:
            b.w = tok
            b.r = {}

    def op(self, eng, meth, *args, reads=(), writes=(), **kw):
        fn = (meth, args, kw)
        waits = self._deps(eng, reads, writes)
        c = self.cnt.get(eng, 0) + 1
        self.cnt[eng] = c
        self.ops[eng].append((waits, fn, (eng, 1)))
        tok = (eng, c)
        self._commit(tok, reads, writes)
        return tok

    def op_nosig(self, eng, meth, *args, reads=(), writes=(), **kw):
        fn = (meth, args, kw)
        waits = self._deps(eng, reads, writes)
        c = self.cnt.get(eng, 0) + 1
        self.ops[eng].append((waits, fn, None))
        tok = (eng, c)
        self._commit(tok, reads, writes)
        return tok

    def dma(self, eng, out, in_, reads=(), writes=(), **kw):
        fn = ("dma_start", (), dict(out=out, in_=in_, **kw))
        waits = self._deps(eng, reads, writes)
        n = self.nd.get(eng, 0)
        self.nd[eng] = n + 1
        key = "d_%s_%d" % (eng, n % NDMA_SEMS)
        if key not in self.dma_keys:
            self.dma_keys.append(key)
            self.eng_keys.setdefault(eng, []).append(key)
        prev = self.cnt.get(key, 0)
        if prev and self.seen[eng].get(key, 0) < prev:
            self.seen[eng][key] = prev
            waits.append((key, prev))
        c = prev + 16
        self.cnt[key] = c
        self.ops[eng].append((waits, fn, (key, 16)))
        tok = (key, c)
        self._commit(tok, reads, writes)
        return tok

    def finish(self):
        for eng, keys in self.eng_keys.items():
            waits = [(k, self.cnt[k]) for k in keys if self.seen[eng].get(k, 0) < self.cnt[k]]
            self.ops[eng].append((waits, None, None))

    def check(self):
        val = {}
        pc = {e: 0 for e in ENGS}
        progress = True
        while progress:
            progress = False
            for e in ENGS:
                ops = self.ops[e]
                while pc[e] < len(ops):
                    waits, fn, sig = ops[pc[e]]
                    if any(val.get(k, 0) < c for k, c in waits):
                        break
                    if sig is not None:
                        val[sig[0]] = val.get(sig[0], 0) + sig[1]
                    pc[e] += 1
                    progress = True
        stuck = {e: (pc[e], len(self.ops[e])) for e in ENGS if pc[e] < len(self.ops[e])}
        if stuck:
            msg = []
            for e, (i, n) in stuck.items():
                waits = self.ops[e][i][0]
                msg.append("%s@%d/%d waits %s have %s" % (e, i, n, waits, [(k, val.get(k, 0)) for k, _ in waits]))
            raise RuntimeError("DEADLOCK: " + "; ".join(msg))

    def emit(self):
        self.check()
        nc = self.nc
        keys = list(ENGS) + self.dma_keys
        with contextlib.ExitStack() as st:
            sems = {k: st.enter_context(nc.semaphore("s_" + k)) for k in keys}
            block = st.enter_context(nc.Block())

            def run(engname):
                def body(e):
                    for waits, fn, sig in self.ops[engname]:
                        for k, c in waits:
                            e.wait_ge(sems[k], c)
                        if fn is not None:
                            ins = getattr(e, fn[0])(*fn[1], **fn[2])
                            if sig is not None:
                                ins.then_inc(sems[sig[0]], sig[1])
                return body

            block.tensor(run("pe"))
            block.scalar(run("act"))
            block.vector(run("dve"))
            block.gpsimd(run("pool"))
            block.sync(run("sp"))


class Builder:
    def __init__(self, seq=8192, past=1024, stages="all"):
        self.SEQ = seq
        self.PAST = past
        self.NS = 4
        self.SD = 32
        self.NT = seq // 512
        self.stages = stages

    def sb(self, name, shape, dt=F32):
        return self.st.enter_context(self.nc.sbuf_tensor(name, shape, dt))[:]

    def din(self, name, shape, dt=F32):
        return self.nc.dram_tensor(name, list(shape), dt, kind="ExternalInput").ap()

    def dout(self, name, shape, dt=F32):
        return self.nc.dram_tensor(name, list(shape), dt, kind="ExternalOutput").ap()

    def dscr(self, name, shape, dt=BF16):
        return self.nc.dram_tensor(name, list(shape), dt, kind="Internal").ap()

    def next_ps(self):
        i = self.ps_i % 4
        self.ps_i += 1
        return self.ps[i], self.b_ps[i]

    def t32(self):
        i = self.r32_i % len(self.b_r32)
        self.r32_i += 1
        return self.r32[:, i, :], self.b_r32[i]

    def t16(self):
        i = self.r16_i % len(self.b_r16)
        self.r16_i += 1
        return self.r16[:, i, :], self.b_r16[i]

    def evac_eng(self):
        self.ev_i += 1
        return "act" if self.ev_i % 2 else "dve"

    def copy(self, eng, out, in_, reads, writes):
        if eng == "act":
            return self.P.op("act", "copy", out, in_, reads=reads, writes=writes)
        return self.P.op(eng, "tensor_copy", out, in_, reads=reads, writes=writes)

    def act(self, out, in_, func, reads, writes, **kw):
        return self.P.op("act", "activation", out, in_, func, reads=reads, writes=writes, **kw)

    def mm(self, out, b_out, terms, allsig=False):
        n = len(terms)
        for i, (l, r, rb) in enumerate(terms):
            last = i == n - 1
            f = self.P.op if (last or allsig) else self.P.op_nosig
            f("pe", "matmul", out, l, r, start=(i == 0), stop=last, reads=rb, writes=[b_out])

    def bigc(self, c, T=512):
        return self.big[:, c, 0:T], [self.b_big[c]]

    def big32(self, q, T=512):
        return self.big32v[:, q, 0:T], [self.b_big[2 * q], self.b_big[2 * q + 1]]

    def dT16(self, c, T=512):
        return self.dT16v[:, c, 0:T], [self.b_dT[c // 2]]

    def conv_weight(self, name, src, K, cols):
        nk = K // 128
        dst = self.dscr(name + "_bf", [len(cols), 128, nk, 128])
        bufs = []
        for f, c0 in enumerate(cols):
            b = Buf("%s_p%d" % (name, f))
            s = src[:, c0:c0 + 128].rearrange("(kc p) c -> p kc c", p=128)
            self.conv_pending.append((dst[f], s, b))
            bufs.append(b)
        return (dst, bufs, nk)

    def flush_conv(self, n=None):
        k = len(self.conv_pending) if n is None else min(n, len(self.conv_pending))
        for (d, s, b) in self.conv_pending[:k]:
            self.P.dma("pool", d, s, writes=[b])
        self.conv_pending = self.conv_pending[k:]

    def panel(self, w, f, k0=0, k1=None):
        dst, bufs, nk = w
        if k1 is None:
            k1 = nk
        i = self.ws_i % len(self.b_ws)
        self.ws_i += 1
        n = k1 - k0
        slot = self.ws[:, i, 0:n * 128].rearrange("p (k c) -> p k c", c=128)
        self.P.dma("sp", slot, dst[f][:, k0:k1, :], reads=[bufs[f]], writes=[self.b_ws[i]])
        return slot, self.b_ws[i]

    def proj_fm(self, w, f, T, k0=0, k1=None, src=None, b_src=None, src_k0=0, c0=0):
        if src is None:
            src, b_src = self.nT, self.b_nT
        slot, b_slot = self.panel(w, f, k0, k1)
        nk = slot.shape[1]
        ps, b_p = self.next_ps()
        self.mm(ps[:, 0:T], b_p,
                [(slot[:, k, :], src[:, src_k0 + k, c0:c0 + T], [b_slot, b_src[src_k0 + k]]) for k in range(nk)])
        return ps, b_p

    def proj_tm1(self, w, f, T):
        slot, b_slot = self.panel(w, f)
        ps, b_p = self.next_ps()
        for nb in range(T // 128):
            self.mm(ps[:, nb * 128:(nb + 1) * 128], b_p,
                    [(self.nT[:, k, nb * 128:(nb + 1) * 128], slot[:, k, :], [b_slot, self.b_nT[k]]) for k in range(KC)])
        return ps, b_p

    def proj_tm4(self, w, fs, T):
        NB = T // 128
        banks = [self.next_ps() for _ in range(NB)]
        for q, f in enumerate(fs):
            slot, b_slot = self.panel(w, f)
            for nb in range(NB):
                ps, b_p = banks[nb]
                self.mm(ps[:, q * 128:(q + 1) * 128], b_p,
                        [(self.nT[:, k, nb * 128:(nb + 1) * 128], slot[:, k, :], [b_slot, self.b_nT[k]]) for k in range(KC)])
        return banks

    def rstd_from(self, chunks, T, ones):
        P = self.P
        ps, b_p = self.next_ps()
        n = len(chunks)
        for k, (ap, bl) in enumerate(chunks):
            sq, b_sq = self.t16()
            self.act(sq[:, 0:T], ap, AF.Square, reads=bl, writes=[b_sq])
            P.op("pe", "matmul", ps[:, 0:T], ones, sq[:, 0:T], start=(k == 0), stop=(k == n - 1),
                 reads=[b_sq, self.b_const], writes=[b_p])
        self.act(self.rstd[:, 0:T], ps[:, 0:T], AF.Sqrt, reads=[b_p], writes=[self.b_rstd], bias=EPS)
        P.op("dve", "reciprocal", self.rstd[:, 0:T], self.rstd[:, 0:T], reads=[self.b_rstd], writes=[self.b_rstd])
        return self.rstd, self.b_rstd

    def norm_to_nT(self, gcol, T):
        P = self.P
        rstd, b_r = self.rstd_from([(self.xT[:, k, 0:T], [self.b_xT[k]]) for k in range(KC)], T, self.onesD)
        for k in range(KC):
            P.op("dve", "scalar_tensor_tensor", self.nT[:, k, 0:T], self.xT[:, k, 0:T], gcol[:, k:k + 1], rstd[:, 0:T],
                 op0=ALU.mult, op1=ALU.mult, reads=[self.b_xT[k], b_r, self.b_const], writes=[self.b_nT[k]])

    def postnorm_add(self, chunks, gcol, T):
        P = self.P
        rstd, b_r = self.rstd_from(chunks, T, self.onesD)
        for k, (ap, bl) in enumerate(chunks):
            tmp, b_t = self.t32()
            P.op("pool", "tensor_tensor", tmp[:, 0:T], ap, rstd[:, 0:T], op=ALU.mult, reads=bl + [b_r], writes=[b_t])
            P.op("dve", "scalar_tensor_tensor", self.xT[:, k, 0:T], tmp[:, 0:T], gcol[:, k:k + 1], self.xT[:, k, 0:T],
                 op0=ALU.mult, op1=ALU.add, reads=[b_t, self.b_xT[k], self.b_const], writes=[self.b_xT[k]])

    def ffn(self, wgu, wd, T):
        P = self.P
        for half in range(2):
            for jj in range(22):
                j = half * 22 + jj
                pg, b_pg = self.proj_fm(wgu, j, T)
                pu, b_pu = self.proj_fm(wgu, NJ + j, T)
                sg, b_sg = self.t32()
                self.act(sg[:, 0:T], pg[:, 0:T], AF.Silu, reads=[b_pg], writes=[b_sg])
                P.op("dve", "tensor_tensor", self.big[:, jj, 0:T], sg[:, 0:T], pu[:, 0:T], op=ALU.mult,
                     reads=[b_sg, b_pu], writes=[self.b_big[jj]])
            for o in range(KC):
                pd, b_pd = self.proj_fm(wd, o, T, k0=half * 22, k1=half * 22 + 22, src=self.big, b_src=self.b_big)
                if half == 0:
                    self.copy("act", self.dT[:, o, 0:T], pd[:, 0:T], [b_pd], [self.b_dT[o]])
                else:
                    P.op("dve", "tensor_tensor", self.dT[:, o, 0:T], self.dT[:, o, 0:T], pd[:, 0:T], op=ALU.add,
                         reads=[b_pd, self.b_dT[o]], writes=[self.b_dT[o]])

    def load_x_tile(self, xrows, T):
        P = self.P
        NB = T // 128
        stage = self.dT.rearrange("p k t -> p (k t)")
        P.dma("sp", stage[:, 0:NB * 2048].rearrange("p (nb f) -> p nb f", f=2048),
              xrows.rearrange("(nb p) f -> p nb f", p=128), writes=self.b_dT)
        for k in range(KC):
            ps, b_p = self.next_ps()
            for nb in range(NB):
                f = P.op if nb == NB - 1 else P.op_nosig
                f("pe", "transpose", ps[:, nb * 128:(nb + 1) * 128],
                  stage[:, nb * 2048 + k * 128: nb * 2048 + (k + 1) * 128], self.ident32,
                  reads=self.b_dT + [self.b_const], writes=[b_p])
            self.copy(self.evac_eng(), self.xT[:, k, 0:T], ps[:, 0:T], [b_p], [self.b_xT[k]])

    def store_y_tile(self, yrows, T):
        P = self.P
        NB = T // 128
        stage = self.dT.rearrange("p k t -> p (k t)")
        for nb in range(NB):
            for g in range(4):
                ps, b_p = self.next_ps()
                for q in range(4):
                    k = g * 4 + q
                    f = P.op if q == 3 else P.op_nosig
                    f("pe", "transpose", ps[:, q * 128:(q + 1) * 128], self.xT[:, k, nb * 128:(nb + 1) * 128], self.ident32,
                      reads=[self.b_xT[k], self.b_const], writes=[b_p])
                self.copy(self.evac_eng(), stage[:, nb * 2048 + g * 512: nb * 2048 + (g + 1) * 512], ps[:, 0:512],
                          [b_p], self.b_dT)
        P.dma("pool", yrows.rearrange("(nb p) f -> p nb f", p=128),
              stage[:, 0:NB * 2048].rearrange("p (nb f) -> p nb f", f=2048), reads=self.b_dT, writes=[Buf("y")])

    def sb_ps(self):
        i = (0, 1, 2, 3, 6, 7)[self.sbps_i % 6]
        self.sbps_i += 1
        return self.ps[i], self.b_ps[i]

    def sb32(self):
        i = self.sb32_i % len(self.sb32_list)
        self.sb32_i += 1
        return self.sb32_list[i]

    def sb16(self):
        i = self.sb16_i % len(self.sb16_list)
        self.sb16_i += 1
        return self.sb16_list[i]

    def sb_attend(self, qT, b_q, Tq, blocks, acc, b_acc, out_ap, out_bufs):
        P = self.P
        C_ = [self.b_const]
        carry, b_c = self.dT[:, 0, 0:Tq], [self.b_dT[0]]
        nblk = len(blocks)
        nk = 128
        st = [dict() for _ in range(nblk)]

        def stage_a(bi):
            d = st[bi]
            blk = blocks[bi]
            if blk[0] == "scr":
                _, h, tj, r, mask = blk
                if r == 3:
                    si = self.kv_i % 3
                    self.kv_i += 1
                    kslot, b_ks = self.bigc(32 + si)
                    vslot, b_vs = self.bigc(35 + si)
                    P.dma("sp", kslot, self.kT_scr[h][:, tj * 512:(tj + 1) * 512], reads=[self.b_kscr[h][tj]], writes=b_ks)
                    P.dma("sp", vslot.rearrange("p (j d) -> p j d", d=128), self.v_scr[h][:, tj * 4:(tj + 1) * 4, :],
                          reads=[self.b_vscr[h][tj]], writes=b_vs)
                    self.cur_kv = (kslot, b_ks, vslot, b_vs)
                kslot, b_ks, vslot, b_vs = self.cur_kv
                kT, bk, v, bv = kslot[:, r * 128:(r + 1) * 128], b_ks, vslot[:, r * 128:(r + 1) * 128], b_vs
            else:
                _, kT, bk, v, bv, mask = blk
            d["v"], d["bv"], d["mask"] = v, bv, mask
            pz, b_pz = self.sb_ps()
            self.mm(pz[0:nk, 0:Tq], b_pz, [(kT, qT, bk + b_q)])
            e32, b_e = self.sb32()
            self.act(e32[0:nk, 0:Tq], pz[0:nk, 0:Tq], AF.Exp, reads=[b_pz], writes=[b_e])
            sp, b_sp = self.sb16()
            self.act(sp[0:nk, 0:Tq], e32[0:nk, 0:Tq], AF.Ln, reads=[b_e], writes=[b_sp], bias=1.0)
            if mask is not None:
                P.op("pool", "tensor_tensor", sp[0:nk, 0:Tq], sp[0:nk, 0:Tq], mask, op=ALU.mult,
                     reads=[b_sp] + C_, writes=[b_sp])
            d["e"], d["be"], d["sp"], d["bsp"] = e32, b_e, sp, b_sp

        def stage_b(bi):
            d = st[bi]
            first, last = bi == 0, bi == nblk - 1
            sp, b_sp = d["sp"], d["bsp"]
            pR, b_pR = self.sb_ps()
            self.mm(pR[0:nk, 0:Tq], b_pR, [(self.negL[0:nk, 0:nk], sp[0:nk, 0:Tq], [b_sp] + C_)])
            if not last:
                pC, b_pC = self.sb_ps()
                self.mm(pC[:, 0:Tq], b_pC, [(self.negones[0:nk, :], sp[0:nk, 0:Tq], [b_sp] + C_)])
            w32, b_w = self.sb32()
            if first:
                self.act(w32[0:nk, 0:Tq], pR[0:nk, 0:Tq], AF.Exp, reads=[b_pR], writes=[b_w])
            else:
                P.op("dve", "tensor_tensor", w32[0:nk, 0:Tq], pR[0:nk, 0:Tq], carry[0:nk, :], op=ALU.add,
                     reads=[b_pR] + b_c, writes=[b_w])
                self.act(w32[0:nk, 0:Tq], w32[0:nk, 0:Tq], AF.Exp, reads=[b_w], writes=[b_w])
            if not last:
                if first:
                    self.copy("dve", carry, pC[:, 0:Tq], [b_pC], b_c)
                else:
                    P.op("dve", "tensor_tensor", carry, carry, pC[:, 0:Tq], op=ALU.add, reads=[b_pC] + b_c, writes=b_c)
            d["w"], d["bw"] = w32, b_w

        def stage_c(bi):
            d = st[bi]
            a16, b_a = self.sb16()
            P.op("pool", "tensor_tensor", a16[0:nk, 0:Tq], d["e"][0:nk, 0:Tq], d["w"][0:nk, 0:Tq], op=ALU.mult,
                 reads=[d["be"], d["bw"]], writes=[b_a])
            if d["mask"] is not None:
                P.op("pool", "tensor_tensor", a16[0:nk, 0:Tq], a16[0:nk, 0:Tq], d["mask"], op=ALU.mult,
                     reads=[b_a] + C_, writes=[b_a])
            d["a"], d["ba"] = a16, b_a

        def stage_d(bi):
            d = st[bi]
            first, last = bi == 0, bi == nblk - 1
            P.op("pe", "matmul", acc, d["v"], d["a"][0:nk, 0:Tq], start=first, stop=last, reads=d["bv"] + [d["ba"]], writes=[b_acc])
            st[bi] = None

        for it in range(nblk + 3):
            if it < nblk:
                stage_a(it)
            if 0 <= it - 3 < nblk:
                stage_d(it - 3)
            if 0 <= it - 2 < nblk:
                stage_c(it - 2)
            if 0 <= it - 1 < nblk:
                stage_b(it - 1)
        self.copy("act", out_ap, acc, [b_acc], out_bufs)

    def sb_stage_prompt(self, ti, T, skp, svp, light=False, orow=0):
        P = self.P
        NB = T // 128
        W = self.W_in
        for h in range(8):
            if True:
                pk, b_pk = self.proj_fm(W, 8 + h, T)
                k16, b_k = self.t16()
                self.copy("act", k16[:, 0:T], pk[:, 0:T], [b_pk], [b_k])
                if True:
                    P.dma("pool", self.kT_scr[h][:, ti * 512: ti * 512 + T], k16[:, 0:T], reads=[b_k], writes=[self.b_kscr[h][ti]])
            for which, fbase, outd in (("k", 8, skp), ("v", 16, svp)):
                if light and which == "k":
                    continue
                pt, b_pt = self.proj_tm1(W, fbase + h, T)
                stg, b_s = self.t32()
                self.copy("dve", stg[:, 0:T], pt[:, 0:T], [b_pt], [b_s])
                if not light:
                    P.dma("pool", outd[orow: orow + T, h * 128:(h + 1) * 128].rearrange("(nb p) d -> p nb d", p=128),
                          stg[:, 0:T].rearrange("p (nb d) -> p nb d", d=128), reads=[b_s], writes=[Buf("o")])
                if which == "v":
                    v16, b_v = self.t16()
                    self.copy("act", v16[:, 0:T], stg[:, 0:T], [b_s], [b_v])
                    P.dma("pool", self.v_scr[h][:, ti * 4: ti * 4 + NB, :], v16[:, 0:T].rearrange("p (nb d) -> p nb d", d=128),
                          reads=[b_v], writes=[self.b_vscr[h][ti]])
        if light:
            return
        self.sb32_list = [(self.dT[:, k, :], self.b_dT[k]) for k in range(1, KC)]
        self.sb16_list = [(self.big[:, c, :], self.b_big[c]) for c in list(range(24, 32)) + [38, 39, 40, 41, 43]]
        for h in range(8):
            pq, b_pq = self.proj_fm(W, h, T)
            qT, b_q = self.bigc(42, T)
            P.op("act", "mul", qT, pq[:, 0:T], 128.0 ** -0.5, reads=[b_pq], writes=b_q)
            blocks = []
            for tj in range(ti, -1, -1):
                for r in range(3, -1, -1):
                    blocks.append(("scr", h, tj, r, self.sbmask[:, r, 0:T] if tj == ti else None))
            acc, b_acc = self.ps[4 + h % 2][:, 0:T], self.b_ps[4 + h % 2]
            o, b_o = self.bigc(h, T)
            self.sb_attend(qT, b_q, T, blocks, acc, b_acc, o, b_o)


    def gla_stage(self, T, sample, sgla=None, glas=None, light=False):
        P = self.P
        NB = T // 128
        W = self.W_in
        C = [self.b_const]
        triN, U, mLE = (self.tri_s, self.U_s, self.mLE_s) if sample else (self.tri_p, self.U_p, self.mLE_p)
        ps, b_p = self.next_ps()
        self.mm(ps[0:16, 0:T], b_p, [(self.wga_sb[:, k, :], self.nT[:, k, 0:T], C + [self.b_nT[k]]) for k in range(KC)])
        gaT, b_ga = self.big32(18)
        self.copy("act", gaT[0:16, 0:T], ps[0:16, 0:T], [b_p], b_ga)
        for nb in range(NB):
            for half in range(2):
                ps, b_p = self.next_ps()
                self.mm(ps[:, 0:512], b_p, [(gaT[0:16, nb * 128:(nb + 1) * 128], self.wa2_sb[0:16, half * 512:(half + 1) * 512], b_ga + C)])
                l, b_l = self.big32(19 + half)
                P.op("dve", "tensor_tensor", l, ps[:, 0:512], self.ba2_bc[:, half * 512:(half + 1) * 512], op=ALU.add,
                     reads=[b_p] + C, writes=b_l)
                self.act(l, l, AF.Exp, reads=b_l, writes=b_l, scale=-1.0)
                self.act(l, l, AF.Ln, reads=b_l, writes=b_l, bias=1.0)
                ps, b_p = self.next_ps()
                for q in range(4):
                    f = P.op if q == 3 else P.op_nosig
                    f("pe", "matmul", ps[:, q * 128:(q + 1) * 128], l[:, q * 128:(q + 1) * 128], triN, start=True, stop=True,
                      reads=b_l + C, writes=[b_p])
                P.op("dve", "tensor_copy", self.dT[:, half * 4:half * 4 + 4, nb * 128:(nb + 1) * 128],
                     ps[:, 0:512].rearrange("p (q t) -> p q t", t=128), reads=[b_p],
                     writes=[self.b_dT[half * 4 + q] for q in range(4)])
                ps2, b_p2 = self.next_ps()
                self.mm(ps2[:, 0:512], b_p2, [(U, l, b_l + C)])
                self.act(self.dT[:, 8 + nb * 2 + half, :], ps2[:, 0:512], AF.Exp, reads=[b_p2], writes=[self.b_dT[8 + nb * 2 + half]])
        for half in range(2):
            banks = self.proj_tm4(W, [32 + half * 4 + q for q in range(4)], T)
            for nb in range(NB):
                ps, b_p = banks[nb]
                P.op("dve", "tensor_tensor", self.big[:, nb * 2 + half, :], ps[:, 0:512], self.dT[:, 8 + nb * 2 + half, :], op=ALU.mult,
                     reads=[b_p, self.b_dT[8 + nb * 2 + half]], writes=[self.b_big[nb * 2 + half]])
        S16 = self.big[:, 33:35, :]
        b_S16 = [self.b_big[33], self.b_big[34]]
        attT, b_att = self.big[:, 32, 0:128], [self.b_big[32]]
        for h in range(4):
            banks = self.proj_tm4(W, [40 + 4 * h + q for q in range(4)], T)
            for nb in range(NB):
                ps, b_p = banks[nb]
                self.copy(self.evac_eng(), self.big[:, 24 + nb, :], ps[:, 0:512], [b_p], [self.b_big[24 + nb]])
            for i in range(2):
                c = 2 * h + i
                if sample:
                    self.act(self.eblast[:, i, 0:4], self.dT[:, c, 31:128:32], AF.Exp, reads=[self.b_dT[c]], writes=[self.b_ebl])
                else:
                    self.act(self.eblast[:, i, 0:NB], self.dT[:, c, 127:T:128], AF.Exp, reads=[self.b_dT[c]], writes=[self.b_ebl])
                if light:
                    continue
                pq, b_pq = self.proj_fm(W, 24 + c, T)
                eb, b_eb = self.t32()
                self.act(eb[:, 0:T], self.dT[:, c, 0:T], AF.Exp, reads=[self.b_dT[c]], writes=[b_eb])
                P.op("dve", "scalar_tensor_tensor", self.big[:, 28 + i, 0:T], pq[:, 0:T], 256.0 ** -0.5, eb[:, 0:T],
                     op0=ALU.mult, op1=ALU.mult, reads=[b_pq, b_eb], writes=[self.b_big[28 + i]])
                pk, b_pk = self.proj_fm(W, 32 + c, T)
                enb, b_enb = self.t32()
                self.act(enb[:, 0:T], self.dT[:, c, 0:T], AF.Exp, reads=[self.b_dT[c]], writes=[b_enb], scale=-1.0)
                P.op("dve", "tensor_tensor", self.big[:, 30 + i, 0:T], pk[:, 0:T], enb[:, 0:T], op=ALU.mult,
                     reads=[b_pk, b_enb], writes=[self.b_big[30 + i]])
            bS = [self.b_S32[2 * h], self.b_S32[2 * h + 1]]
            for ch in range(NB):
                cs = slice(ch * 128, (ch + 1) * 128)
                if not light:
                    ps, b_p = self.next_ps()
                    self.mm(ps[:, 0:128], b_p, [(self.big[:, 30 + i, cs], self.big[:, 28 + i, cs], [self.b_big[30 + i], self.b_big[28 + i]])
                                                for i in range(2)])
                    P.op("dve", "tensor_tensor", attT, ps[:, 0:128], mLE, op=ALU.mult, reads=[b_p] + C, writes=b_att)
                segs = [(sq * 32, 32, sq) for sq in range(4)] if sample else [(ch * 128, 128, ch)]
                for (c0, n, si) in segs:
                    cq = slice(c0, c0 + n)
                    lq = slice(c0 - ch * 128, c0 - ch * 128 + n)
                    if sample:
                        P.dma("sp", self.S32[:, 2 * h:2 * h + 2, :], sgla[si][2 * h:2 * h + 2].rearrange("j p d -> p j d"), writes=bS)
                    for j in range(2):
                        if not light:
                            self.copy("act", S16[:, j, :], self.S32[:, 2 * h + j, :], [bS[j]], [b_S16[j]])
                    for dvc in range(4):
                        if light:
                            break
                        dv = slice(dvc * 128, (dvc + 1) * 128)
                        terms = [(S16[:, j, dv], self.big[:, 28 + j, cq], [b_S16[j], self.b_big[28 + j]]) for j in range(2)]
                        terms.append((self.big[:, 24 + ch, dv], attT[:, lq], [self.b_big[24 + ch]] + b_att))
                        self.mm(self.ps[4 + dvc][:, cq], self.b_ps[4 + dvc], terms)
                    cell = ch * 2 + (2 * h) // 4
                    off = ((2 * h) % 4) * 128
                    if sample:
                        kd, b_kd = self.big[:, 35, 0:256], [self.b_big[35]]
                        P.op("dve", "tensor_scalar_mul", kd, self.big[:, cell, off:off + 256], self.seqmask[:, si:si + 1],
                             reads=[self.b_big[cell]] + C, writes=b_kd)
                    else:
                        kd, b_kd = self.big[:, cell, off:off + 256], [self.b_big[cell]]
                    for j in range(2):
                        ps, b_p = self.next_ps()
                        self.mm(ps[:, 0:512], b_p, [(kd[:, j * 128:(j + 1) * 128], self.big[:, 24 + ch, :], b_kd + [self.b_big[24 + ch]])])
                        P.op("dve", "scalar_tensor_tensor", self.S32[:, 2 * h + j, :], self.S32[:, 2 * h + j, :],
                             self.eblast[:, j, si:si + 1], ps[:, 0:512], op0=ALU.mult, op1=ALU.add,
                             reads=[bS[j], self.b_ebl, b_p], writes=[bS[j]])
                    if sample:
                        P.dma("pool", glas[si][2 * h:2 * h + 2].rearrange("j p d -> p j d"), self.S32[:, 2 * h:2 * h + 2, :],
                              reads=bS, writes=[Buf("o")])
            if light:
                continue
            rstd, b_r = self.rstd_from([(self.ps[4 + dvc][:, 0:T], [self.b_ps[4 + dvc]]) for dvc in range(4)], T, self.ones512)
            for dvc in range(4):
                pr, b_pr = self.proj_fm(W, 56 + 4 * h + dvc, T)
                sr, b_sr = self.t32()
                self.act(sr[:, 0:T], pr[:, 0:T], AF.Silu, reads=[b_pr], writes=[b_sr])
                tmp, b_t = self.t32()
                P.op("dve", "scalar_tensor_tensor", tmp[:, 0:T], self.ps[4 + dvc][:, 0:T], self.glag_sb[:, dvc:dvc + 1], rstd[:, 0:T],
                     op0=ALU.mult, op1=ALU.mult, reads=[self.b_ps[4 + dvc], b_r] + C, writes=[b_t])
                P.op("pool", "tensor_tensor", self.big[:, 8 + 4 * h + dvc, 0:T], tmp[:, 0:T], sr[:, 0:T], op=ALU.mult,
                     reads=[b_t, b_sr], writes=[self.b_big[8 + 4 * h + dvc]])

    def mem_prologue(self, mem, memk, memv):
        P = self.P
        W = self.W_memkv
        self.load_x_tile(mem, 256)
        self.norm_to_nT(self.G(6), 256)
        for c in range(8):
            pk, b_pk = self.proj_fm(W, c, 256)
            self.copy(self.evac_eng(), self.mkT16[:, c, :], pk[:, 0:256], [b_pk], [self.b_mk])
        for grp, outd in ((0, memk), (1, memv)):
            for half in range(2):
                banks = self.proj_tm4(W, [grp * 8 + half * 4 + q for q in range(4)], 256)
                for nb in range(2):
                    ps, b_p = banks[nb]
                    stg, b_s = self.t32()
                    self.copy("act", stg, ps[:, 0:512], [b_p], [b_s])
                    P.dma("pool", outd[nb * 128:(nb + 1) * 128, half * 512:(half + 1) * 512], stg, reads=[b_s], writes=[Buf("o")])
                    if grp == 1:
                        self.copy("dve", self.mv16[:, nb, half * 512:(half + 1) * 512], stg, [b_s], [self.b_mv])

    def mem_stage(self, T, sample, cmk=None, cmv=None):
        P = self.P
        W = self.W_in
        C = [self.b_const]
        for c in range(8):
            pq, b_pq = self.proj_fm(W, 72 + c, T)
            P.op("act", "mul", self.big[:, 32 + c, 0:T], pq[:, 0:T], 1.0 / 16.0, reads=[b_pq], writes=[self.b_big[32 + c]])
        segs = [(sq * 32, 32, sq) for sq in range(4)] if sample else [(0, T, None)]
        for (c0, n, sq) in segs:
            if sample:
                stage = self.dT[:, 0:4, :].rearrange("p (b k) t -> p b (k t)", b=2)
                bst = self.b_dT[0:4]
                P.dma("sp", stage, cmk[sq].rearrange("(b p) f -> p b f", p=128), writes=bst)
                for c in range(8):
                    ps, b_p = self.next_ps()
                    for mb in range(2):
                        f = P.op if mb == 1 else P.op_nosig
                        f("pe", "transpose", ps[:, mb * 128:(mb + 1) * 128], stage[:, mb, c * 128:(c + 1) * 128], self.ident32,
                          reads=bst + C, writes=[b_p])
                    self.copy(self.evac_eng(), self.mkT16[:, c, :], ps[:, 0:256], [b_p], [self.b_mk])
                P.dma("pool", self.mv16, cmv[sq].rearrange("(b p) f -> p b f", p=128), writes=[self.b_mv])
            for h in range(4):
                p16 = []
                for mb in range(2):
                    ps, b_p = self.next_ps()
                    self.mm(ps[:, 0:n], b_p, [(self.mkT16[:, 2 * h + i, mb * 128:(mb + 1) * 128], self.big[:, 32 + 2 * h + i, c0:c0 + n],
                                               [self.b_mk, self.b_big[32 + 2 * h + i]]) for i in range(2)])
                    pt, b_pt = self.t16()
                    self.act(pt[:, 0:n], ps[:, 0:n], AF.Exp, reads=[b_p], writes=[b_pt])
                    p16.append((pt, b_pt))
                ps, b_p = self.next_ps()
                self.mm(ps[:, 0:n], b_p, [(self.onesM, pt[:, 0:n], C + [b_pt]) for (pt, b_pt) in p16])
                rden, b_rd = self.t32()
                P.op("dve", "reciprocal", rden[:, 0:n], ps[:, 0:n], reads=[b_p], writes=[b_rd])
                for i in range(2):
                    ps, b_p = self.next_ps()
                    self.mm(ps[:, 0:n], b_p, [(self.mv16[:, mb, (2 * h + i) * 128:(2 * h + i + 1) * 128], p16[mb][0][:, 0:n],
                                               [self.b_mv, p16[mb][1]]) for mb in range(2)])
                    P.op("dve", "tensor_tensor", self.big[:, 24 + 2 * h + i, c0:c0 + n], ps[:, 0:n], rden[:, 0:n], op=ALU.mult,
                         reads=[b_p, b_rd], writes=[self.b_big[24 + 2 * h + i]])

    def merge_stage(self, T):
        P = self.P
        C = [self.b_const]
        branches = [(self.W_sbbr, 0), (self.W_glabr, 8), (self.W_membr, 24)]
        macc, b_m = self.dT[:, 8, 0:T], [self.b_dT[8]]
        for j in range(KC):
            for i, (Wb, cell0) in enumerate(branches):
                pg, b_pg = self.proj_fm(self.W_gate, i * 16 + j, T)
                g, b_g = self.t32()
                self.act(g[:, 0:T], pg[:, 0:T], AF.Sigmoid, reads=[b_pg] + C, writes=[b_g], bias=self.bgate_sb[:, i * 16 + j:i * 16 + j + 1])
                pb, b_pb = self.proj_fm(Wb, j, T, src=self.big, b_src=self.b_big, src_k0=cell0)
                if i == 0:
                    P.op("dve", "tensor_tensor", macc, g[:, 0:T], pb[:, 0:T], op=ALU.mult, reads=[b_g, b_pb], writes=b_m)
                else:
                    t, b_t = self.t32()
                    P.op("dve", "tensor_tensor", t[:, 0:T], g[:, 0:T], pb[:, 0:T], op=ALU.mult, reads=[b_g, b_pb], writes=[b_t])
                    if i == 1:
                        P.op("pool", "tensor_tensor", macc, macc, t[:, 0:T], op=ALU.add, reads=b_m + [b_t], writes=b_m)
                    else:
                        o, b_o = self.dT16(j, T)
                        P.op("pool", "tensor_tensor", o, macc, t[:, 0:T], op=ALU.add, reads=b_m + [b_t], writes=b_o)
        b16 = [self.b_dT[c // 2] for c in range(KC)]
        outs = []
        for o in range(KC):
            pm, b_pm = self.proj_fm(self.W_out, o, T, src=self.dT16v, b_src=b16)
            dst, b_d = self.big32(o, T)
            self.copy(self.evac_eng(), dst, pm[:, 0:T], [b_pm], b_d)
            outs.append((dst, b_d))
        self.postnorm_add(outs, self.G(3), T)

    def sb_stage_sample(self, sks, svs, csk, csv):
        P = self.P
        T = 128
        W = self.W_in
        C = [self.b_const]
        self.sb32_list = [(self.dT[:, k, :], self.b_dT[k]) for k in range(4, KC)]
        self.sb16_list = [(self.big[:, c, :], self.b_big[c]) for c in list(range(24, 32)) + [40, 41, 43]]
        for h in range(8):
            pk, b_pk = self.proj_fm(W, 8 + h, T)
            kTn, b_kn = self.bigc(38, T)
            self.copy("act", kTn, pk[:, 0:T], [b_pk], b_kn)
            pt, b_pt = self.proj_tm1(W, 8 + h, T)
            stg, b_s = self.t32()
            self.copy("dve", stg[:, 0:128], pt[:, 0:128], [b_pt], [b_s])
            P.dma("pool", sks[:, h * 128:(h + 1) * 128], stg[:, 0:128], reads=[b_s], writes=[Buf("o")])
            pv, b_pv = self.proj_tm1(W, 16 + h, T)
            stg, b_s = self.t32()
            self.copy("dve", stg[:, 0:128], pv[:, 0:128], [b_pv], [b_s])
            P.dma("pool", svs[:, h * 128:(h + 1) * 128], stg[:, 0:128], reads=[b_s], writes=[Buf("o")])
            vN, b_vn = self.bigc(39, 128)
            self.copy("act", vN, stg[:, 0:128], [b_s], b_vn)
            pq, b_pq = self.proj_fm(W, h, T)
            qT, b_q = self.bigc(42, T)
            P.op("act", "mul", qT, pq[:, 0:T], 128.0 ** -0.5, reads=[b_pq], writes=b_q)
            for sq in range(4):
                stage = self.dT[:, 2:4, :].rearrange("p a (b d) -> p (a b) d", d=128)
                bst = self.b_dT[2:4]
                P.dma("sp", stage, csk[sq][:, h * 128:(h + 1) * 128].rearrange("(kb p) d -> p kb d", p=128), writes=bst)
                for g in range(2):
                    ps, b_p = self.next_ps()
                    for q in range(4):
                        f = P.op if q == 3 else P.op_nosig
                        f("pe", "transpose", ps[:, q * 128:(q + 1) * 128], stage[:, g * 4 + q, :], self.ident32, reads=bst + C, writes=[b_p])
                    self.copy(self.evac_eng(), self.big[:, 32 + g, :], ps[:, 0:512], [b_p], [self.b_big[32 + g]])
                vP = self.big[:, 35:37, :].rearrange("p a (b d) -> p (a b) d", d=128)
                bvp = self.b_big[35:37]
                P.dma("pool", vP, csv[sq][:, h * 128:(h + 1) * 128].rearrange("(kb p) d -> p kb d", p=128), writes=bvp)
                blocks = [("sb", kTn, b_kn, vN, b_vn, self.sbmask_s[:, sq, :])]
                for kb in range(7, -1, -1):
                    blocks.append(("sb", self.big[:, 32 + kb // 4, (kb % 4) * 128:(kb % 4 + 1) * 128], [self.b_big[32 + kb // 4]],
                                   vP[:, kb, :], bvp, None))
                ai = 4 + (h * 4 + sq) % 2
                self.sb_attend(qT[:, sq * 32:(sq + 1) * 32], b_q, 32, blocks, self.ps[ai][:, 0:32], self.b_ps[ai],
                               self.big[:, h, sq * 32:(sq + 1) * 32], [self.b_big[h]])

    def mask_sel(self, ap, cm, coef, n, base, op):
        self.P.op("pool", "affine_select", out=ap, in_=ap, compare_op=op, fill=0.0, base=base, pattern=[[coef, n]],
                  channel_multiplier=cm, reads=[self.b_const], writes=[self.b_const])

    def build(self):
        nc = bass.Bass("TRN2", target_bir_lowering=False)
        self.nc = nc
        self.P = Prog(nc)
        P = self.P
        SEQ, PAST, NS = self.SEQ, self.PAST, self.NS
        C = None
        with contextlib.ExitStack() as st:
            self.st = st
            OWN = SEQ // 2
            xp = self.din("xp", [OWN, D]); xpre = self.din("xpre", [OWN, D]); xs = self.din("xs", [128, D]); mem = self.din("mem", [256, D])
            csk = self.din("csk", [NS, PAST, 1024]); csv = self.din("csv", [NS, PAST, 1024])
            sgla = self.din("sgla", [NS, 8, 128, 512])
            cmk = self.din("cmk", [NS, 256, 1024]); cmv = self.din("cmv", [NS, 256, 1024])
            w_gu1 = self.din("w_gu1", [D, 2 * DFF]); w_d1 = self.din("w_d1", [DFF, D])
            w_gu2 = self.din("w_gu2", [D, 2 * DFF]); w_d2 = self.din("w_d2", [DFF, D])
            w_in = self.din("w_in", [D, 10256])
            w_a2 = self.din("w_a2", [16, 1024]); ba2 = self.din("ba2", [128, 1024])
            w_memkv = self.din("w_memkv", [D, 2048])
            w_sbbr = self.din("w_sbbr", [1024, D]); w_glabr = self.din("w_glabr", [2048, D]); w_membr = self.din("w_membr", [1024, D])
            w_gate = self.din("w_gate", [D, 3 * D]); w_out = self.din("w_out", [D, D])
            gains = self.din("gains", [128, 7 * 16]); bgate = self.din("bgate", [128, 48]); glag = self.din("glag", [128, 4])
            yp = self.dout("yp", [OWN, D]); ys = self.dout("ys", [128, D])
            skp = self.dout("skp", [OWN, 1024]); svp = self.dout("svp", [OWN, 1024])
            glap = self.dout("glap", [8, 128, 512])
            memk = self.dout("memk", [256, 1024]); memv = self.dout("memv", [256, 1024])
            sks = self.dout("sks", [128, 1024]); svs = self.dout("svs", [128, 1024])
            glas = self.dout("glas", [NS, 8, 128, 512])
            self.kT_scr = self.dscr("kT_scr", [8, 128, SEQ])
            self.v_scr = self.dscr("v_scr", [8, 128, SEQ // 128, 128])
            self.b_kscr = [[Buf("kscr") for _ in range(self.NT)] for _ in range(8)]
            self.b_vscr = [[Buf("vscr") for _ in range(self.NT)] for _ in range(8)]
            self.xT = self.sb("xT", [128, KC, 512]); self.b_xT = [Buf("xT%d" % k) for k in range(KC)]
            self.nT = self.sb("nT", [128, KC, 512], BF16); self.b_nT = [Buf("nT%d" % k) for k in range(KC)]
            self.big = self.sb("big", [128, NJ, 512], BF16); self.b_big = [Buf("big%d" % k) for k in range(NJ)]
            self.big32v = self.big.rearrange("p c t -> p (c t)").bitcast(F32).rearrange("p (c t) -> p c t", t=512)
            self.dT = self.sb("dT", [128, KC, 512]); self.b_dT = [Buf("dT%d" % k) for k in range(KC)]
            self.dT16v = self.dT.rearrange("p c t -> p (c t)").bitcast(BF16).rearrange("p (c t) -> p c t", t=512)
            NWS = 5
            self.ws = self.sb("ws", [128, NWS, 22 * 128], BF16); self.b_ws = [Buf("ws%d" % k) for k in range(NWS)]
            self.r32 = self.sb("r32", [128, 4, 512]); self.b_r32 = [Buf("r32_%d" % k) for k in range(4)]
            self.r16 = self.sb("r16", [128, 4, 512], BF16); self.b_r16 = [Buf("r16_%d" % k) for k in range(4)]
            self.rstd = self.sb("rstd", [128, 512]); self.b_rstd = Buf("rstd")
            self.S32 = self.sb("S32", [128, 8, 512]); self.b_S32 = [Buf("S32_%d" % k) for k in range(8)]
            self.mkT16 = self.sb("mkT16", [128, 8, 256], BF16); self.b_mk = Buf("mk")
            self.mv16 = self.sb("mv16", [128, 2, 1024], BF16); self.b_mv = Buf("mv")
            self.eblast = self.sb("eblast", [128, 2, 4]); self.b_ebl = Buf("ebl")
            self.ident32 = self.sb("ident32", [128, 128])
            self.onesD = self.sb("onesD", [128, 128], BF16)
            self.ones512 = self.sb("ones512", [128, 128], BF16)
            self.onesM = self.sb("onesM", [128, 128], BF16)
            self.negones = self.sb("negones", [128, 128], BF16)
            self.negL = self.sb("negL", [128, 128], BF16)
            self.sbmask = self.sb("sbmask", [128, 4, 512], BF16)
            self.sbmask_s = self.sb("sbmask_s", [128, 4, 32], BF16)
            self.seqmask = self.sb("seqmask", [128, 4])
            self.tri_p = self.sb("tri_p", [128, 128]); self.U_p = self.sb("U_p", [128, 128]); self.mLE_p = self.sb("mLE_p", [128, 128])
            self.tri_s = self.sb("tri_s", [128, 128]); self.U_s = self.sb("U_s", [128, 128]); self.mLE_s = self.sb("mLE_s", [128, 128])
            self.gcols = self.sb("gcols", [128, 7 * 16])
            self.bgate_sb = self.sb("bgate_sb", [128, 48])
            self.glag_sb = self.sb("glag_sb", [128, 4])
            self.ba2_bc = self.sb("ba2_bc", [128, 1024])
            self.wa2_sb = self.sb("wa2_sb", [16, 1024])
            self.wga_sb = self.sb("wga_sb", [128, 16, 16], BF16)
            self.b_const = Buf("const")
            C = [self.b_const]
            self.ps = [st.enter_context(nc.psum_tensor("ps%d" % i, [128, 512], F32))[:] for i in range(8)]
            self.b_ps = [Buf("ps%d" % i) for i in range(8)]
            self.ps_i = self.r32_i = self.r16_i = self.ws_i = self.ev_i = self.kv_i = self.sbps_i = self.sb32_i = self.sb16_i = 0
            self.G = lambda i: self.gcols[:, i * 16:(i + 1) * 16]

            def ms(ap, v):
                P.op("pool", "memset", ap, v, writes=C)
            GE, GT = ALU.is_ge, ALU.is_gt
            ms(self.ident32, 0.0)
            P.op("pool", "affine_select", out=self.ident32, in_=self.ident32, compare_op=ALU.not_equal, fill=1.0, base=0,
                 pattern=[[-1, 128]], channel_multiplier=1, reads=C, writes=C)
            ms(self.onesD, 1.0 / D); ms(self.ones512, 1.0 / 512); ms(self.onesM, 1.0); ms(self.negones, -1.0)
            ms(self.negL, -1.0); self.mask_sel(self.negL, 1, -1, 128, 0, GE)
            for r in range(4):
                ms(self.sbmask[:, r, :], 1.0); self.mask_sel(self.sbmask[:, r, :], -1, 1, 512, -128 * r, GT)
            for sq in range(4):
                ms(self.sbmask_s[:, sq, :], 1.0)
                self.mask_sel(self.sbmask_s[:, sq, :], 1, 0, 32, -32 * sq, GE)
                self.mask_sel(self.sbmask_s[:, sq, :], -1, 1, 32, 32 * sq, GT)
                ms(self.seqmask[:, sq:sq + 1], 1.0)
                self.mask_sel(self.seqmask[:, sq:sq + 1], 1, 0, 1, -32 * sq, GE)
                self.mask_sel(self.seqmask[:, sq:sq + 1], -1, 0, 1, 32 * sq + 31, GE)
            for (tri, U, mLE, blk) in ((self.tri_p, self.U_p, self.mLE_p, False), (self.tri_s, self.U_s, self.mLE_s, True)):
                ms(tri, -1.0 / 16); self.mask_sel(tri, -1, 1, 128, 0, GE)
                ms(U, -1.0 / 16); self.mask_sel(U, 1, -1, 128, 0, GT)
                ms(mLE, 1.0); self.mask_sel(mLE, -1, 1, 128, 0, GE)
                if blk:
                    for m in (tri, U, mLE):
                        for bt in range(4):
                            sl = m[:, bt * 32:(bt + 1) * 32]
                            self.mask_sel(sl, 1, 0, 32, -32 * bt, GE)
                            self.mask_sel(sl, -1, 0, 32, 32 * bt + 31, GE)
            P.dma("sp", self.gcols, gains, writes=C)
            P.dma("sp", self.bgate_sb, bgate, writes=C)
            P.dma("sp", self.glag_sb, glag, writes=C)
            P.dma("sp", self.ba2_bc, ba2, writes=C)
            P.dma("sp", self.wa2_sb, w_a2, writes=C)
            for gi in (1, 5):
                P.op("act", "mul", self.G(gi), self.G(gi), 0.5, reads=C, writes=C)
            for k in range(8):
                P.op("pool", "memset", self.S32[:, k, :], 0.0, writes=[self.b_S32[k]])

            self.conv_pending = []
            c128 = lambda n: [128 * i for i in range(n)]
            self.W_memkv = self.conv_weight("memkv", w_memkv, D, c128(16))
            self.W_gu1 = self.conv_weight("gu1", w_gu1, D, c128(88))
            self.W_d1 = self.conv_weight("d1", w_d1, DFF, c128(16))
            self.flush_conv()
            wga_scr = self.dscr("wga_scr", [128, 16, 16])
            b_wga = Buf("wga")
            P.dma("pool", wga_scr, w_in[:, 9216:9232].rearrange("(kc p) c -> p kc c", p=128), writes=[b_wga])
            P.dma("sp", self.wga_sb, wga_scr, reads=[b_wga], writes=C)
            self.W_in = self.conv_weight("win", w_in, D, c128(72) + [9232 + 128 * i for i in range(8)])
            self.mem_prologue(mem, memk, memv)
            self.flush_conv()
            self.W_gate = self.conv_weight("gate", w_gate, D, c128(48))
            self.W_sbbr = self.conv_weight("sbbr", w_sbbr, 1024, c128(16))
            self.W_glabr = self.conv_weight("glabr", w_glabr, 2048, c128(16))
            self.W_membr = self.conv_weight("membr", w_membr, 1024, c128(16))
            self.W_out = self.conv_weight("out", w_out, D, c128(16))
            self.W_gu2 = self.conv_weight("gu2", w_gu2, D, c128(88))
            self.W_d2 = self.conv_weight("d2", w_d2, DFF, c128(16))

            NH = self.NT // 2
            per_tile = (len(self.conv_pending) + max(NH - 1, 1) - 1) // max(NH - 1, 1)
            tiles = [(i, xpre[i * 512:(i + 1) * 512, :], None, 512, False, True, 0) for i in range(NH)]
            tiles += [(NH + i, xp[i * 512:(i + 1) * 512, :], yp[i * 512:(i + 1) * 512, :], 512, False, False, i * 512) for i in range(NH)]
            tiles.append((self.NT, xs, ys, 128, True, False, 0))
            for (ti, xin, yout, T, sample, light, orow) in tiles:
                self.load_x_tile(xin, T)
                self.norm_to_nT(self.G(0), T)
                self.ffn(self.W_gu1, self.W_d1, T)
                self.postnorm_add([(self.dT[:, k, 0:T], [self.b_dT[k]]) for k in range(KC)], self.G(1), T)
                self.norm_to_nT(self.G(2), T)
                self.gla_stage(T, sample, sgla, glas, light=light)
                if sample:
                    self.sb_stage_sample(sks, svs, csk, csv)
                else:
                    self.sb_stage_prompt(ti, T, skp, svp, light=light, orow=orow)
                if light:
                    self.flush_conv(per_tile)
                    continue
                self.flush_conv()
                self.mem_stage(T, sample, cmk, cmv)
                self.merge_stage(T)
                if ti == self.NT - 1:
                    P.dma("pool", glap.rearrange("c p d -> p c d"), self.S32, reads=self.b_S32, writes=[Buf("o")])
                self.norm_to_nT(self.G(4), T)
                self.ffn(self.W_gu2, self.W_d2, T)
                self.postnorm_add([(self.dT[:, k, 0:T], [self.b_dT[k]]) for k in range(KC)], self.G(5), T)
                self.store_y_tile(yout, T)
            P.finish()
            P.emit()
        return nc


def _colmat(v, n):
    return np.ascontiguousarray(np.asarray(v, np.float32).reshape(n, 128).T)


def kernel(**inp):
    f = lambda k: np.asarray(inp[k], np.float32)
    x_prompt, x_sample, mem_prompt = f("x_prompt"), f("x_sample"), f("mem_prompt")
    B, SEQ, _ = x_prompt.shape
    csk, csv, sg = f("cache_sb_k")[0], f("cache_sb_v")[0], f("state_gla")[0]
    cmk, cmv = f("cache_mem_k")[0], f("cache_mem_v")[0]
    PAST = csk.shape[1]
    gains = np.concatenate([_colmat(f(k)[0], 16) for k in
                            ("ffn1_pre_g", "ffn1_post_g", "mix_pre_g", "mix_post_g", "ffn2_pre_g", "ffn2_post_g", "mem_norm_g")], axis=1)
    shared = dict(
        w_gu1=f("ffn1_w_gu")[0], w_d1=f("ffn1_w_d")[0], w_gu2=f("ffn2_w_gu")[0], w_d2=f("ffn2_w_d")[0], w_in=f("w_in")[0],
        w_a2=f("gla_w_a2")[0], ba2=np.ascontiguousarray(np.broadcast_to(f("gla_b_a2")[0][None, :], (128, 1024))),
        w_memkv=f("w_mem_kv")[0], w_sbbr=f("w_sb_br")[0], w_glabr=f("w_gla_br")[0], w_membr=f("w_mem_br")[0],
        w_gate=f("w_gate")[0], w_out=f("w_out")[0], gains=np.ascontiguousarray(gains),
        bgate=_colmat(f("b_gate")[0], 48), glag=_colmat(f("gla_norm_g")[0], 4))
    in_maps = []
    OWN = SEQ // 2
    zeros = np.zeros((OWN, D), np.float32)
    for c in range(8):
        b, half = (c // 2) % B, c % 2
        s = slice(4 * c, 4 * c + 4)
        m = dict(shared)
        m.update(xp=np.ascontiguousarray(x_prompt[b, half * OWN:(half + 1) * OWN]),
                 xpre=(np.ascontiguousarray(x_prompt[b, 0:OWN]) if half else zeros),
                 xs=np.ascontiguousarray(x_sample[s].reshape(128, D)), mem=mem_prompt[b],
                 csk=np.ascontiguousarray(csk[s].reshape(4, PAST, 1024)), csv=np.ascontiguousarray(csv[s].reshape(4, PAST, 1024)),
                 sgla=np.ascontiguousarray(sg[s].reshape(4, 8, 128, 512)),
                 cmk=np.ascontiguousarray(cmk[s].reshape(4, 256, 1024)), cmv=np.ascontiguousarray(cmv[s].reshape(4, 256, 1024)))
        in_maps.append(m)
    nc = Builder(seq=SEQ, past=PAST).build()
    res = run_bass_kernel_spmd(nc, in_maps, core_ids=list(range(8))).results
    cat = lambda k, b: np.concatenate([res[2 * b][k], res[2 * b + 1][k]], 0)
    y_p = np.stack([cat("yp", b) for b in range(B)])
    y_s = np.concatenate([res[c]["ys"].reshape(4, 32, D) for c in range(8)])
    sk_p = np.stack([cat("skp", b).reshape(SEQ, 8, 128) for b in range(B)])[None]
    sv_p = np.stack([cat("svp", b).reshape(SEQ, 8, 128) for b in range(B)])[None]
    gla_p = np.stack([res[2 * b + 1]["glap"].reshape(4, 256, 512) for b in range(B)])[None]
    mk_p = np.stack([res[2 * b]["memk"].reshape(256, 4, 256) for b in range(B)])[None]
    mv_p = np.stack([res[2 * b]["memv"].reshape(256, 4, 256) for b in range(B)])[None]
    sk_s = np.concatenate([res[c]["sks"].reshape(4, 32, 8, 128) for c in range(8)])[None]
    sv_s = np.concatenate([res[c]["svs"].reshape(4, 32, 8, 128) for c in range(8)])[None]
    gla_s = np.concatenate([res[c]["glas"].reshape(4, 4, 256, 512) for c in range(8)])[None]
    return (y_p, y_s, sk_p, sv_p, gla_p, mk_p, mv_p, sk_s, sv_s, gla_s)
```
